# Optimizing a Trainium2 kernel written in Bass

```python
import math
import jax, jax.numpy as jnp
from jax import lax
import numpy as np

D_MODEL = 1024
BATCH = 8
SEQ = 4096
DEPTH = 1

DIFF_HEADS = 4
DIFF_HEAD_DIM = 64
DIFF_V_DIM = 2 * DIFF_HEAD_DIM
DIFF_QK_COLS = DIFF_HEADS * 2 * DIFF_HEAD_DIM
DIFF_V_COLS = DIFF_HEADS * DIFF_V_DIM
MLA_HEADS = 4
MLA_Q_RANK = 384
MLA_KV_RANK = 256
MLA_NOPE_DIM = 128
MLA_ROPE_DIM = 64
MLA_V_DIM = 128
MLA_V_COLS = MLA_HEADS * MLA_V_DIM
ROPE_BASE = 10000.0
IN_COLS = 2 * DIFF_QK_COLS + DIFF_V_COLS + MLA_Q_RANK + MLA_KV_RANK + MLA_ROPE_DIM
MIX_WIDTH = DIFF_V_COLS + MLA_V_COLS
D_FF = 4 * D_MODEL
REL_BUCKETS = 32
REL_MAX_DIST = 128
Q_BLOCK = 128
NORM_EPS = 1e-6
NEG_INF = -1e30

kernel_name = "hymba_diffattn_mla_sqrelu_block"


def rms_norm(x, gain):
    x32 = x.astype(jnp.float32)
    y = x32 * lax.rsqrt(jnp.mean(x32 * x32, axis=-1, keepdims=True) + NORM_EPS)
    return (y * gain.astype(jnp.float32)).astype(x.dtype)


def t5_bucket(dist):
    n = jnp.maximum(dist, 0)
    max_exact = REL_BUCKETS // 2
    nf = jnp.maximum(n, 1).astype(jnp.float32)
    large = max_exact + (jnp.log(nf / max_exact) / math.log(REL_MAX_DIST / max_exact)
                         * (REL_BUCKETS - max_exact)).astype(jnp.int32)
    large = jnp.minimum(large, REL_BUCKETS - 1)
    return jnp.where(n < max_exact, n, large)


def apply_rope(x, positions):
    inv_freq = ROPE_BASE ** (-jnp.arange(0, MLA_ROPE_DIM, 2, dtype=jnp.float32) / MLA_ROPE_DIM)
    ang = positions.astype(jnp.float32)[:, None] * inv_freq[None, :]
    cos, sin = jnp.cos(ang), jnp.sin(ang)
    x32 = x.astype(jnp.float32)
    x1, x2 = jnp.split(x32, 2, axis=-1)
    out = jnp.concatenate([x1 * cos - x2 * sin, x1 * sin + x2 * cos], axis=-1)
    return out.astype(x.dtype)


def diff_attention(q, k, v, positions, rel_bias, lam, lambda_init, subln):
    B, S = q.shape[0], q.shape[1]
    nb = S // Q_BLOCK
    scale = DIFF_HEAD_DIM ** -0.5
    qb = q.reshape(B, nb, Q_BLOCK, DIFF_HEADS, 2, DIFF_HEAD_DIM).transpose(1, 0, 3, 4, 2, 5)
    kt = k.transpose(0, 2, 3, 1, 4)
    vt = v.transpose(0, 2, 1, 3)
    pos_blocks = positions.reshape(nb, Q_BLOCK)

    def block(args):
        q_blk, q_pos = args
        logits = jnp.einsum('bhcqd,bhckd->bhcqk', q_blk, kt,
                            preferred_element_type=jnp.float32) * scale
        dist = q_pos[:, None] - positions[None, :]
        bias = jnp.transpose(rel_bias[t5_bucket(dist)], (2, 0, 1))
        logits = logits + bias.astype(jnp.float32)[None, :, None]
        logits = jnp.where((dist >= 0)[None, None, None], logits, NEG_INF)
        probs = jax.nn.softmax(logits, axis=-1)
        attn = probs[:, :, 0] - lam * probs[:, :, 1]
        return jnp.einsum('bhqk,bhkd->bhqd', attn.astype(vt.dtype), vt)

    out = lax.map(block, (qb, pos_blocks))
    out = out.transpose(1, 0, 3, 2, 4).reshape(B, S, DIFF_HEADS, DIFF_V_DIM)
    out = rms_norm(out, subln) * (1.0 - lambda_init)
    return out.reshape(B, S, DIFF_V_COLS)


def mla_attention(c_q, c_kv, k_pe, positions, q_norm, w_uq, kv_norm, w_ukv):
    B, S = c_q.shape[0], c_q.shape[1]
    nb = S // Q_BLOCK
    q = (rms_norm(c_q, q_norm) @ w_uq).reshape(B, S, MLA_HEADS, MLA_NOPE_DIM + MLA_ROPE_DIM)
    q = q.transpose(0, 2, 1, 3)
    q_nope, q_pe = q[..., :MLA_NOPE_DIM], q[..., MLA_NOPE_DIM:]
    q_pe = apply_rope(q_pe, positions)
    kv = (rms_norm(c_kv, kv_norm) @ w_ukv).reshape(B, S, MLA_HEADS, MLA_NOPE_DIM + MLA_V_DIM)
    kv = kv.transpose(0, 2, 1, 3)
    k_nope, v = kv[..., :MLA_NOPE_DIM], kv[..., MLA_NOPE_DIM:]
    k_pe = apply_rope(k_pe, positions)
    scale = (MLA_NOPE_DIM + MLA_ROPE_DIM) ** -0.5
    qn_b = q_nope.reshape(B, MLA_HEADS, nb, Q_BLOCK, MLA_NOPE_DIM).transpose(2, 0, 1, 3, 4)
    qp_b = q_pe.reshape(B, MLA_HEADS, nb, Q_BLOCK, MLA_ROPE_DIM).transpose(2, 0, 1, 3, 4)
    pos_blocks = positions.reshape(nb, Q_BLOCK)

    def block(args):
        qn, qp, q_pos = args
        logits = (jnp.einsum('bhqd,bhkd->bhqk', qn, k_nope, preferred_element_type=jnp.float32)
                  + jnp.einsum('bhqr,bkr->bhqk', qp, k_pe, preferred_element_type=jnp.float32)) * scale
        mask = positions[None, :] <= q_pos[:, None]
        logits = jnp.where(mask[None, None], logits, NEG_INF)
        probs = jax.nn.softmax(logits, axis=-1)
        return jnp.einsum('bhqk,bhkd->bhqd', probs.astype(v.dtype), v)

    out = lax.map(block, (qn_b, qp_b, pos_blocks))
    return out.transpose(1, 0, 3, 2, 4).reshape(B, S, MLA_V_COLS)


def setup_inputs(seed: int = 0) -> dict:
    key = jax.random.key(seed)
    ks = jax.random.split(key, 24)
    f32 = jnp.float32

    def nrm(k, shape, scale):
        return jax.random.normal(k, shape, f32) * scale

    def gain(k, shape):
        return 1.0 + 0.02 * jax.random.normal(k, shape, f32)

    return {
        "x": jax.random.normal(ks[0], (BATCH, SEQ, D_MODEL), f32),
        "positions": jnp.arange(SEQ, dtype=jnp.int32),
        "rel_bias": nrm(ks[1], (REL_BUCKETS, DIFF_HEADS), 0.1),
        "norm_attn": gain(ks[2], (DEPTH, D_MODEL)),
        "w_in": nrm(ks[3], (DEPTH, D_MODEL, IN_COLS), D_MODEL ** -0.5),
        "diff_lq1": nrm(ks[4], (DEPTH, DIFF_HEAD_DIM), 0.1),
        "diff_lk1": nrm(ks[5], (DEPTH, DIFF_HEAD_DIM), 0.1),
        "diff_lq2": nrm(ks[6], (DEPTH, DIFF_HEAD_DIM), 0.1),
        "diff_lk2": nrm(ks[7], (DEPTH, DIFF_HEAD_DIM), 0.1),
        "diff_subln": gain(ks[8], (DEPTH, DIFF_V_DIM)),
        "mla_q_norm": gain(ks[9], (DEPTH, MLA_Q_RANK)),
        "mla_w_uq": nrm(ks[10], (DEPTH, MLA_Q_RANK, MLA_HEADS * (MLA_NOPE_DIM + MLA_ROPE_DIM)), MLA_Q_RANK ** -0.5),
        "mla_kv_norm": gain(ks[11], (DEPTH, MLA_KV_RANK)),
        "mla_w_ukv": nrm(ks[12], (DEPTH, MLA_KV_RANK, MLA_HEADS * (MLA_NOPE_DIM + MLA_V_DIM)), MLA_KV_RANK ** -0.5),
        "w_out": nrm(ks[13], (DEPTH, MIX_WIDTH, D_MODEL), MIX_WIDTH ** -0.5),
        "norm_mlp": gain(ks[14], (DEPTH, D_MODEL)),
        "w_mlp_in": nrm(ks[15], (DEPTH, D_MODEL, D_FF), D_MODEL ** -0.5),
        "w_mlp_out": nrm(ks[16], (DEPTH, D_FF, D_MODEL), D_FF ** -0.5),
        "norm_final": gain(ks[17], (D_MODEL,)),
    }


def reference(x, positions, rel_bias, norm_attn, w_in, diff_lq1, diff_lk1, diff_lq2, diff_lk2,
              diff_subln, mla_q_norm, mla_w_uq, mla_kv_norm, mla_w_ukv, w_out, norm_mlp,
              w_mlp_in, w_mlp_out, norm_final):
    B, S = x.shape[0], x.shape[1]
    splits = [DIFF_QK_COLS, 2 * DIFF_QK_COLS, 2 * DIFF_QK_COLS + DIFF_V_COLS,
              2 * DIFF_QK_COLS + DIFF_V_COLS + MLA_Q_RANK,
              2 * DIFF_QK_COLS + DIFF_V_COLS + MLA_Q_RANK + MLA_KV_RANK]
    for l in range(DEPTH):
        lambda_init = 0.8 - 0.6 * math.exp(-0.3 * l)
        h = rms_norm(x, norm_attn[l])
        proj = h @ w_in[l]
        dq, dk, dv, c_q, c_kv, k_pe = jnp.split(proj, splits, axis=-1)
        dq = dq.reshape(B, S, DIFF_HEADS, 2, DIFF_HEAD_DIM)
        dk = dk.reshape(B, S, DIFF_HEADS, 2, DIFF_HEAD_DIM)
        dv = dv.reshape(B, S, DIFF_HEADS, DIFF_V_DIM)
        lam = (jnp.exp(jnp.sum(diff_lq1[l].astype(jnp.float32) * diff_lk1[l].astype(jnp.float32)))
               - jnp.exp(jnp.sum(diff_lq2[l].astype(jnp.float32) * diff_lk2[l].astype(jnp.float32)))
               + lambda_init)
        out_a = diff_attention(dq, dk, dv, positions, rel_bias, lam, lambda_init, diff_subln[l])
        out_b = mla_attention(c_q, c_kv, k_pe, positions, mla_q_norm[l], mla_w_uq[l],
                              mla_kv_norm[l], mla_w_ukv[l])
        x = x + jnp.concatenate([out_a, out_b], axis=-1) @ w_out[l]
        h = rms_norm(x, norm_mlp[l])
        x = x + jnp.square(jax.nn.relu(h @ w_mlp_in[l])) @ w_mlp_out[l]
    return rms_norm(x, norm_final)
```

```python
import math
import os
from contextlib import ExitStack

import numpy as np
import ml_dtypes

import concourse.bass as bass
import concourse.mybir as mybir
from concourse.bass_utils import run_bass_kernel_spmd

F32 = mybir.dt.float32
BF16 = mybir.dt.bfloat16
I32 = mybir.dt.int32
ACT = mybir.ActivationFunctionType
ALU = mybir.AluOpType
AX = mybir.AxisListType

S_LEN = 4096
D_MODEL = 1024
NT = S_LEN // 128
NQ = S_LEN // 512
EPS = 1e-6
TAB_L = 1151


class Res:
    __slots__ = ("name", "psum", "w", "r", "rd")

    def __init__(self, name, psum=False):
        self.name = name
        self.psum = psum
        self.w = None
        self.r = {}
        self.rd = []


class Op:
    __slots__ = ("eng", "fn", "idx", "deps_c", "deps_d", "signal", "is_dma",
                 "dsem", "dval", "prevval", "sval", "tag")

    def __init__(self, eng, fn, is_dma, tag=None):
        self.eng = eng
        self.fn = fn
        self.is_dma = is_dma
        self.deps_c = {}
        self.deps_d = []
        self.signal = False
        self.dsem = None
        self.dval = 0
        self.prevval = 0
        self.sval = 0
        self.idx = -1
        self.tag = tag


class Sched:
    ENGS = ("pe", "act", "dve", "pool", "sp")

    def __init__(self, ndma_slots=None):
        self.streams = {e: [] for e in self.ENGS}
        self.nslots = ndma_slots or {"sp": 12, "pool": 40}
        self.last_c = {}
        self.dmas_since = []
        self.pending = {}

    def _dep(self, op, d, kind):
        if d is None or d is op:
            return
        if (not d.is_dma) and (not op.is_dma) and d.eng == op.eng:
            if op.eng == "pe" or kind == "bar":
                return
        if d.is_dma:
            if d not in op.deps_d:
                op.deps_d.append(d)
        else:
            cur = op.deps_c.get(d.eng)
            if cur is None or d.idx > cur.idx:
                op.deps_c[d.eng] = d

    def _record(self, op, reads, writes):
        writes = list(writes)
        rds = []
        for R in reads:
            if R.psum:
                if R not in writes:
                    writes.append(R)
            else:
                rds.append(R)
        pend = self.pending.pop(op.eng, None)
        if pend is not None:
            for d in pend:
                self._dep(op, d, "bar")
        for R in rds:
            self._dep(op, R.w, "raw")
        for R in writes:
            self._dep(op, R.w, "waw")
            for rd in R.r.values():
                self._dep(op, rd, "war")
            for rd in R.rd:
                self._dep(op, rd, "war")
        for R in rds:
            if R in writes:
                continue
            if op.is_dma:
                R.rd.append(op)
            else:
                R.r[op.eng] = op
        for R in writes:
            R.w = op
            R.r = {}
            R.rd = []
        st = self.streams[op.eng]
        op.idx = len(st)
        st.append(op)
        if op.is_dma:
            self.dmas_since.append(op)
        else:
            self.last_c[op.eng] = op
        return op

    def op(self, eng, fn, reads=(), writes=(), tag=None):
        return self._record(Op(eng, fn, False, tag), reads, writes)

    def dma(self, queue, fn, reads=(), writes=(), tag=None):
        return self._record(Op(queue, fn, True, tag), reads, writes)

    def barrier(self):
        deps = list(self.last_c.values()) + list(self.dmas_since)
        for e in self.ENGS:
            old = self.pending.get(e)
            self.pending[e] = (old or []) + deps if old else list(deps)
        self.dmas_since = []

    def finalize(self):
        for e in self.ENGS:
            for op in self.streams[e]:
                for d in op.deps_c.values():
                    d.signal = True
        self.stats = {}
        for e in self.ENGS:
            cnt = 0
            i = 0
            n = self.nslots.get(e, 1)
            for op in self.streams[e]:
                if op.is_dma:
                    op.dsem = (e, i % n)
                    op.dval = 16 * (i // n + 1)
                    op.prevval = 16 * (i // n)
                    i += 1
                elif op.signal:
                    cnt += 1
                    op.sval = cnt
            self.stats[e] = (len(self.streams[e]), cnt, i)

    def emit_engine(self, e, eng, csems, dsems):
        waited = {}
        nwaits = 0
        for op in self.streams[e]:
            waits = []
            for d in op.deps_c.values():
                waits.append((("c", d.eng), csems[d.eng], d.sval))
            for d in op.deps_d:
                waits.append((("d",) + d.dsem, dsems[d.dsem], d.dval))
            if op.is_dma and op.prevval > 0:
                waits.append((("d",) + op.dsem, dsems[op.dsem], op.prevval))
            need = {}
            for key, sem, val in waits:
                if waited.get(key, 0) < val and need.get(key, (None, 0))[1] < val:
                    need[key] = (sem, val)
            need = list(need.items())
            emb = None
            if need and not op.is_dma:
                emb = need.pop()
            for key, (sem, val) in need:
                eng.wait_ge(sem, val)
                waited[key] = val
                nwaits += 1
            ins = op.fn(eng)
            if emb is not None:
                ins._wait_ge(emb[1][0], emb[1][1])
                waited[emb[0]] = emb[1][1]
            if op.is_dma:
                ins.then_inc(dsems[op.dsem], 16)
            elif op.signal:
                ins.then_inc(csems[e], 1)
        if e == "sp":
            for q in self.ENGS:
                for op in self.streams[q]:
                    if op.is_dma and op.tag == "out":
                        key = ("d",) + op.dsem
                        if waited.get(key, 0) < op.dval:
                            eng.wait_ge(dsems[op.dsem], op.dval)
                            waited[key] = op.dval
        return nwaits


def _t5_bucket_np(n):
    n = np.maximum(n, 0)
    max_exact = 16
    nf = np.maximum(n, 1).astype(np.float32)
    large = max_exact + (np.log(nf / np.float32(max_exact)) / np.float32(math.log(128 / 16))
                         * np.float32(32 - max_exact)).astype(np.int32)
    large = np.minimum(large, 31)
    return np.where(n < max_exact, n, large)


def _onehot_table():
    d = np.arange(TAB_L) - 511
    oh = np.zeros((33, TAB_L), np.float32)
    b = _t5_bucket_np(d)
    for i in range(TAB_L):
        if d[i] >= 0:
            oh[b[i], i] = 1.0
        else:
            oh[32, i] = 1.0
    return oh


def build_program(stop_after=None, dump_mix=False, dumps=()):
    nc = bass.Bass("TRN2", target_bir_lowering=False)
    S = Sched()
    es = ExitStack()

    def dbg_dump(name, ap, ncols, dt):
        if name not in dumps:
            return
        dd = nc.dram_tensor("dbg_" + name, [128, ncols], dt, kind="ExternalOutput")
        S.barrier()
        S.dma("sp", lambda e: e.dma_start(out=dd[:, :], in_=ap), [], [], tag="out")
        S.barrier()

    def din(name, shape, dt):
        return nc.dram_tensor(name, shape, dt, kind="ExternalInput")

    x_d = din("x", [S_LEN, D_MODEL], F32)
    w_in_d = din("w_in", [1024, 2240], F32)
    w_uq_d = din("w_uq", [384, 768], F32)
    w_ukv_d = din("w_ukv", [256, 1024], F32)
    w_out_d = din("w_out", [1024, 1024], F32)
    w1_d = din("w1", [1024, 4096], F32)
    w2_d = din("w2", [4096, 1024], F32)
    gains_d = din("gains", [128, 22], F32)
    gfin_d = din("gfin", [1, 1024], F32)
    relb_d = din("relb", [32, 4], F32)
    lam_d = din("lamv", [1, 256], F32)
    pos_d = din("pos", [128, 32], I32)
    invf_d = din("invf", [1, 32], F32)
    oh_d = din("oh", [33, TAB_L], F32)
    ident_d = din("ident", [128, 128], BF16)
    antiid_d = din("antiid", [128, 128], BF16)
    y_d = nc.dram_tensor("y", [S_LEN, D_MODEL], F32, kind="ExternalOutput")
    scr_d = nc.dram_tensor("scr", [5, TAB_L], BF16, kind="Internal")
    w1b_d = nc.dram_tensor("w1b", [1024, 4096], BF16, kind="Internal")
    w2b_d = nc.dram_tensor("w2b", [4096, 1024], BF16, kind="Internal")
    wob_d = nc.dram_tensor("wob", [1024, 1024], BF16, kind="Internal")
    mixdump_d = None
    if dump_mix:
        mixdump_d = nc.dram_tensor("mixdump", [128, 8 * S_LEN], BF16, kind="ExternalOutput")

    with es:
        def sb(name, shape, dt):
            return es.enter_context(nc.sbuf_tensor(name, shape, dt))

        ident = sb("ident_sb", [128, 128], BF16)
        antiid = sb("antiid_sb", [128, 128], BF16)
        ones = sb("ones_sb", [128, 128], BF16)
        c32 = sb("c32_sb", [128, 128], F32)
        gains = sb("gains_sb", [128, 22], F32)
        small = sb("small_sb", [128, 16], F32)
        rstd_all = sb("rstd_all_sb", [128, 32], F32)
        r_rstd = [Res(f"rstd{t}") for t in range(32)]
        strips = sb("strips_sb", [128, 5, 1024], BF16)
        mixT = sb("mixT_sb", [128, 8, S_LEN], BF16)
        ARENA_BYTES = 128 * 1024
        arena = sb("arena_sb", [128, ARENA_BYTES // 2], BF16)
        r_const = Res("const")
        r_small = Res("small")
        r_strips = Res("strips")
        r_mix = [[Res(f"mix{c}_{j}") for j in range(NQ)] for c in range(8)]

        banks = [es.enter_context(nc.psum_tensor(f"bank{i}", [128, 512], F32)) for i in range(8)]
        r_bank = [Res(f"bank{i}", True) for i in range(8)]

        def bank_bf(i):
            return banks[i][:, :].bitcast(BF16)

        class Arena:
            def __init__(self, base=None, nbytes=None):
                self.off = 0
                self.base = base
                self.nbytes = nbytes

            def reset(self, off=0):
                self.off = off

            def alloc(self, nbytes, dt, shape3=None):
                off = (self.off + 31) // 32 * 32
                self.off = off + nbytes
                if self.base is None:
                    assert self.off <= ARENA_BYTES, f"arena overflow {self.off}"
                    ap = arena[:, off // 2:(off + nbytes) // 2]
                else:
                    assert self.off <= self.nbytes, f"arena overflow {self.off}"
                    ap = self.base[:, off // 2:(off + nbytes) // 2]
                if dt == F32:
                    ap = ap.bitcast(F32)
                elif dt == I32:
                    ap = ap.bitcast(I32)
                if shape3 is not None:
                    ap = ap.rearrange("p (a b) -> p a b", a=shape3[0], b=shape3[1])
                return ap

        A = Arena()

        def mm(out, lhsT, rhs, start, stop, reads, writes):
            S.op("pe", lambda e: e.matmul(out, lhsT=lhsT, rhs=rhs, start=start, stop=stop), reads, writes)

        def tr(out, in_, reads, writes):
            S.op("pe", lambda e: e.transpose(out=out, in_=in_, identity=ident[:, :]), list(reads) + [r_const], writes)

        def act(out, in_, func, reads, writes, **kw):
            S.op("act", lambda e: e.activation(out=out, in_=in_, func=func, **kw), reads, writes)

        def tt(eng, out, in0, in1, op, reads, writes):
            S.op(eng, lambda e: e.tensor_tensor(out=out, in0=in0, in1=in1, op=op), reads, writes)

        def ts(eng, out, in0, s1, s2, op0, op1, reads, writes):
            if s2 is None:
                S.op(eng, lambda e: e.tensor_scalar(out=out, in0=in0, scalar1=s1, scalar2=None, op0=op0), reads, writes)
            else:
                S.op(eng, lambda e: e.tensor_scalar(out=out, in0=in0, scalar1=s1, scalar2=s2, op0=op0, op1=op1), reads, writes)

        def stt(eng, out, in0, scalar, in1, op0, op1, reads, writes):
            S.op(eng, lambda e: e.scalar_tensor_tensor(out=out, in0=in0, scalar=scalar, in1=in1, op0=op0, op1=op1),
                 reads, writes)

        def cp(eng, out, in_, reads, writes):
            if eng == "act":
                act(out, in_, ACT.Copy, reads, writes)
            else:
                S.op(eng, lambda e: e.tensor_copy(out=out, in_=in_), reads, writes)

        def recip(eng, out, in_, reads, writes):
            S.op(eng, lambda e: e.reciprocal(out=out, in_=in_), reads, writes)

        def dma(q, out, in_, reads, writes, tag=None):
            S.dma(q, lambda e: e.dma_start(out=out, in_=in_), reads, writes, tag=tag)

        def rsqrt_act(out, in_, n, reads, writes, tmp, r_tmp):
            act(tmp, in_, ACT.Ln, list(reads), [r_tmp], scale=1.0 / n, bias=EPS)
            act(out, tmp, ACT.Exp, [r_tmp], writes, scale=-0.5)

        A0 = Arena(mixT[:, :, :].rearrange("p a b -> p (a b)"), 8 * S_LEN * 2)
        A_main = A
        A = A0
        oh_sb = A.alloc(TAB_L * 4, F32)
        relb_aug = A.alloc(32, F32)
        lamt = A.alloc(256 * 4, F32)
        prod = A.alloc(128 * 4, F32)
        etab = A.alloc(1152 * 2, BF16)
        hank = [A.alloc(1024 * 2, BF16) for _ in range(2)]
        posi = A.alloc(32 * 4, I32)
        posf = A.alloc(32 * 4, F32)
        invf = A.alloc(32 * 4, F32)
        ang = A.alloc(1024 * 4, F32, (32, 32))
        u_t = A.alloc(1024 * 4, F32, (32, 32))
        k_i = A.alloc(1024 * 4, I32, (32, 32))
        k_f = A.alloc(1024 * 4, F32, (32, 32))
        fr = A.alloc(1024 * 4, F32, (32, 32))
        stage0_end = 0
        A = A_main
        ROPE_OFF = ARENA_BYTES - 2 * 8192
        A.reset(ROPE_OFF)
        C2 = A.alloc(8192, F32, (32, 64))
        S2 = A.alloc(8192, F32, (32, 64))
        assert stage0_end <= ROPE_OFF
        r_oh, r_relb, r_lamt, r_prod, r_etab = Res("oh"), Res("relb"), Res("lamt"), Res("prod"), Res("etab")
        r_hank = [Res("hank0"), Res("hank1")]
        r_pos, r_posf, r_invf, r_ang, r_u, r_ki, r_kf, r_fr = (Res(n) for n in
                                                                ("pos", "posf", "invf", "ang", "u", "ki", "kf", "fr"))
        r_C2, r_S2, r_scr = Res("C2"), Res("S2"), Res("scr")

        dma("sp", ident[:, :], ident_d[:, :], [], [r_const])
        dma("sp", antiid[:, :], antiid_d[:, :], [], [r_const])
        S.op("pool", lambda e: e.memset(ones[:, :], 1.0), [], [r_const])
        S.op("pool", lambda e: e.memset(c32[:, :], 1.0 / 32.0), [], [r_const])
        dma("sp", gains[:, :], gains_d[:, :], [], [r_const])
        dma("sp", oh_sb[0:33, :], oh_d[:, :], [], [r_oh])
        S.op("pool", lambda e: e.memset(relb_aug[0:33, 0:8], 0.0), [], [r_relb])
        S.op("pool", lambda e: e.memset(relb_aug[32:33, 0:8], -30000.0), [], [r_relb])
        dma("sp", relb_aug[0:32, 0:4], relb_d[:, :], [], [r_relb])
        dma("sp", small[:, 0:4], bass.AP(relb_d, 31 * 4, [[0, 128], [1, 4]]), [], [r_small])
        dma("sp", lamt, bass.AP(lam_d, 0, [[0, 128], [1, 256]]), [], [r_lamt])
        dma("sp", posi, pos_d[:, :], [], [r_pos])
        dma("sp", invf, bass.AP(invf_d, 0, [[0, 128], [1, 32]]), [], [r_invf])

        tt("dve", prod, lamt[:, 0:128], lamt[:, 128:256], ALU.mult, [r_lamt], [r_prod])
        S.op("dve", lambda e: e.tensor_reduce(out=small[:, 8:10], in_=prod.rearrange("p (a b) -> p a b", a=2, b=64),
                                              axis=AX.X, op=ALU.add), [r_prod], [r_small])
        act(small[:, 6:8], small[:, 8:10], ACT.Exp, [r_small], [r_small])
        tt("dve", small[:, 4:5], small[:, 7:8], small[:, 6:7], ALU.subtract, [r_small], [r_small])
        ts("dve", small[:, 4:5], small[:, 4:5], -0.2, None, ALU.add, None, [r_small], [r_small])
        ts("dve", small[:, 5:6], gains[:, 21:22], 0.8, None, ALU.mult, None, [r_const, r_small], [r_small])

        for ci, (c0, c1) in enumerate(((0, 512), (512, 1024), (1024, TAB_L))):
            mm(banks[ci][0:5, 0:c1 - c0], relb_aug[0:33, 0:5], oh_sb[0:33, c0:c1], True, True,
               [r_relb, r_oh], [r_bank[ci]])
            act(etab[0:5, c0:c1], banks[ci][0:5, 0:c1 - c0], ACT.Exp, [r_bank[ci]], [r_etab])
        dma("sp", scr_d[:, :], etab[0:5, 0:TAB_L], [r_etab], [r_scr])
        for h in range(5):
            hk = hank[h % 2]
            dma("sp", hk, bass.AP(scr_d, h * TAB_L, [[1, 128], [1, 1024]]), [r_scr], [r_hank[h % 2]])
            for half in range(2):
                bi = 3 + (2 * h + half) % 4
                mm(banks[bi][:, :], antiid[:, :], hk[:, half * 512:(half + 1) * 512], True, True,
                   [r_const, r_hank[h % 2]], [r_bank[bi]])
                cp("act" if half else "dve", strips[:, h, half * 512:(half + 1) * 512], banks[bi][:, :],
                   [r_bank[bi]], [r_strips])

        cp("dve", posf, posi, [r_pos], [r_posf])
        tt("dve", ang, invf.unsqueeze(1).to_broadcast([128, 32, 32]), posf.unsqueeze(2).to_broadcast([128, 32, 32]),
           ALU.mult, [r_invf, r_posf], [r_ang])
        TWO_PI_S = 6.283184
        for kind in ("sin", "cos"):
            if kind == "sin":
                ts("dve", u_t, ang, 1.0 / (2 * math.pi), None, ALU.mult, None, [r_ang], [r_u])
            else:
                ts("dve", u_t, ang, 1.0 / (2 * math.pi), 0.25, ALU.mult, ALU.add, [r_ang], [r_u])
            cp("dve", k_i, u_t, [r_u], [r_ki])
            cp("dve", k_f, k_i, [r_ki], [r_kf])
            tt("dve", fr, u_t, k_f, ALU.subtract, [r_u, r_kf], [r_fr])
            if kind == "sin":
                act(S2[:, :, 32:64], fr, ACT.Sin, [r_fr], [r_S2], scale=TWO_PI_S)
                act(S2[:, :, 0:32], fr, ACT.Sin, [r_fr], [r_S2], scale=-TWO_PI_S)
            else:
                act(C2[:, :, 0:32], fr, ACT.Sin, [r_fr], [r_C2], scale=TWO_PI_S)
                act(C2[:, :, 32:64], fr, ACT.Sin, [r_fr], [r_C2], scale=TWO_PI_S)

        g_attn = gains[:, 0:8]
        g_mlp = gains[:, 8:16]
        g_q = gains[:, 16:19]
        g_kv = gains[:, 19:21]

        STRIP_ENG = "dve"

        SRING = [0, 1, 2, 7]
        sb_ctr = [0]

        def next_sb():
            b = SRING[sb_ctr[0] % 4]
            sb_ctr[0] += 1
            return b

        def skew(nt, stages):
            ns = len(stages)
            for it in range(nt + ns - 1):
                for s in range(ns - 1, -1, -1):
                    t = it - s
                    if 0 <= t < nt:
                        stages[s](t)

        def run_attention(units, PT, r_PT):
            steps = [(ui, c) for ui, u in enumerate(units) for c in range(4 * u["j"] + 4)]
            DEPTH = 3
            NPT = len(PT)
            sbank_of = {}
            deferred = []

            def lo_of(u, c):
                m = c - 4 * u["j"]
                return 128 * m if (m > 0 and u["j"] > 0) else 0

            def emit_S(i):
                ui, c = steps[i]
                u = units[ui]
                lo = lo_of(u, c)
                bk = next_sb()
                parts = u["kparts"]
                for pi, (kfn, qap, qres) in enumerate(parts):
                    kap, kres = kfn(c)
                    mm(banks[bk][:, lo:512], kap, qap[:, lo:512], pi == 0, pi == len(parts) - 1,
                       [kres, qres], [r_bank[bk]])
                near = c >= 4 * u["j"] + u["nearmin"]
                pt = PT[i % NPT]
                if near or u["bias"] is None:
                    act(pt[:, lo:512], banks[bk][:, lo:512], ACT.Exp, [r_bank[bk]], [r_PT[i % NPT]], scale=u["scale"])
                else:
                    act(pt[:, lo:512], banks[bk][:, lo:512], ACT.Exp, [r_bank[bk], r_small], [r_PT[i % NPT]],
                        scale=u["scale"], bias=u["bias"])
                if near:
                    delta = 512 * u["j"] - 128 * c
                    st = strips[:, u["strip"], delta + 384 + lo:delta + 384 + 512]
                    tt(STRIP_ENG, pt[:, lo:512], pt[:, lo:512], st, ALU.mult, [r_PT[i % NPT], r_strips], [r_PT[i % NPT]])

            def emit_PV(i):
                ui, c = steps[i]
                u = units[ui]
                lo = lo_of(u, c)
                first = c == 0
                last = c == 4 * u["j"] + 3
                ob = 3 + 2 * (ui % 2)
                db = ob + 1
                vap, vres = u["v"](c)
                pt = PT[i % NPT]
                mm(banks[ob][:, lo:512], vap, pt[:, lo:512], first, last, [vres, r_PT[i % NPT]], [r_bank[ob]])
                if c % 4 == 3:
                    for g in range(4):
                        ii = i - 3 + g
                        cc = c - 3 + g
                        lo2 = lo_of(u, cc)
                        p2 = PT[ii % NPT]
                        S.op("pe", lambda e, g=g, lo2=lo2, p2=p2, st=(cc < 4), sp=(cc >= 4 * u["j"]), db=db: e.matmul(
                            banks[db][32 * g:32 * g + 32, lo2:512], lhsT=ones[:, 0:32], rhs=p2[:, lo2:512], start=st, stop=sp,
                            tile_position=(0, 32 * g)), [r_const, r_PT[ii % NPT]], [r_bank[db]])
                if last:
                    u["fin"](u, ob, db, lambda fn, k=3: deferred.append((i + k, fn)))

            n = len(steps)
            task_at = {}
            base = 0
            for ui, u in enumerate(units):
                ns_ = 4 * u["j"] + 4
                tk = u.get("tasks", [])
                for k, fn in enumerate(tk):
                    task_at.setdefault(base + (k * ns_) // len(tk), []).append(fn)
                base += ns_
            for i in range(n + DEPTH):
                if i < n:
                    emit_S(i)
                    for fn in task_at.get(i, []):
                        fn()
                if i >= DEPTH:
                    emit_PV(i - DEPTH)
                    due = [d for d in deferred if d[0] <= i - DEPTH]
                    for d in due:
                        deferred.remove(d)
                        d[1]()
            for d in deferred:
                d[1]()

        A.reset()
        c_qT = A.alloc(3 * S_LEN * 2, BF16, (3, S_LEN))
        c_kvT = A.alloc(2 * S_LEN * 2, BF16, (2, S_LEN))
        k_ropeT = A.alloc(S_LEN * 2, BF16)
        q_ropeT = A.alloc(2 * S_LEN * 2, BF16, (2, S_LEN))
        BOUT_END = A.off
        r_cq = [Res(f"cq{j}") for j in range(NQ)]
        r_ckv = [Res(f"ckv{j}") for j in range(NQ)]
        r_krope = [Res(f"krope{j}") for j in range(NQ)]
        r_qrope = [Res(f"qrope{j}") for j in range(NQ)]

        xs = [A.alloc(4096, F32) for _ in range(3)]
        r_xs = [Res(f"xs{i}") for i in range(3)]
        junk = A.alloc(2048, BF16)
        r_junk = Res("junk")
        xn = [A.alloc(2048, BF16) for _ in range(2)]
        r_xn = [Res(f"xn{i}") for i in range(2)]
        hTt = [A.alloc(2048, BF16, (8, 128)) for _ in range(2)]
        r_hTt = [Res(f"hTt{i}") for i in range(2)]
        Wlat = A.alloc(8 * 768 * 2, BF16, (8, 768))
        r_Wlat = Res("Wlat")
        Wqr = A.alloc(3 * 512 * 2, BF16, (3, 512))
        r_Wqr = Res("Wqr")
        latn = [A.alloc(768 * 2, BF16) for _ in range(2)]
        r_latn = [Res(f"latn{i}") for i in range(2)]
        stat = [A.alloc(64, F32) for _ in range(2)]
        r_stat = [Res(f"stat{i}") for i in range(2)]
        rtmp = [A.alloc(3 * 256, F32) for _ in range(2)]
        r_rtmp = [Res(f"rtmp{i}") for i in range(2)]
        qtmp = [A.alloc(2 * 1024, F32) for _ in range(2)]
        r_qtmp = [Res(f"qtmp{i}") for i in range(2)]
        qper = [A.alloc(512, BF16) for _ in range(2)]
        r_qper = [Res(f"qper{i}") for i in range(2)]
        assert A.off <= ROPE_OFF, A.off

        def wview(wd, c0, c1):
            return wd[:, c0:c1].rearrange("(k p) c -> p k c", p=128)

        dma("pool", Wlat[:, :, 0:704], wview(w_in_d, 1536, 2240), [], [r_Wlat])
        dma("pool", Wlat[:, :, 704:736], wview(w_in_d, 2208, 2240), [], [r_Wlat])
        dma("pool", Wlat[:, :, 736:768], wview(w_in_d, 2176, 2208), [], [r_Wlat])
        for h in range(4):
            b = h * 192 + 128
            dma("pool", Wqr[:, :, h * 64:(h + 1) * 64], wview(w_uq_d, b, b + 64), [], [r_Wqr])
            dma("pool", Wqr[:, :, 256 + h * 64:256 + h * 64 + 32], wview(w_uq_d, b + 32, b + 64), [], [r_Wqr])
            dma("pool", Wqr[:, :, 256 + h * 64 + 32:256 + h * 64 + 64], wview(w_uq_d, b, b + 32), [], [r_Wqr])

        def norm_stages(gain_ap, dst_fn, xs, r_xs, xn, r_xn, junk, r_junk, stat, r_stat, tp_banks, reuse_rstd=False):
            def st_load(t):
                b = t % len(xs)
                dma("sp", xs[b], x_d[t * 128:(t + 1) * 128, :], [], [r_xs[b]])

            def st_stat(t):
                b = t % len(xs)
                sa = stat[t % 2]
                act(junk, xs[b], ACT.Square, [r_xs[b]], [r_junk, r_stat[t % 2]], accum_out=sa[:, 0:1])
                act(sa[:, 1:2], sa[:, 0:1], ACT.Ln, [r_stat[t % 2]], [r_stat[t % 2]], scale=1.0 / 1024.0, bias=EPS)
                act(rstd_all[:, t:t + 1], sa[:, 1:2], ACT.Exp, [r_stat[t % 2]], [r_rstd[t]], scale=-0.5)

            def st_xn(t):
                b = t % len(xs)
                ts("dve", xn[t % 2], xs[b], rstd_all[:, t:t + 1], None, ALU.mult, None, [r_xs[b], r_rstd[t]], [r_xn[t % 2]])

            def st_tr(t):
                bk = tp_banks[t % len(tp_banks)]
                tpv = bank_bf(bk).rearrange("p (a b) -> p a b", a=8, b=128)
                for c in range(8):
                    tr(tpv[:, c, :], xn[t % 2][:, c * 128:(c + 1) * 128], [r_xn[t % 2]], [r_bank[bk]])

            def st_evac(t):
                bk = tp_banks[t % len(tp_banks)]
                tpv = bank_bf(bk).rearrange("p (a b) -> p a b", a=8, b=128)
                dst, rdst = dst_fn(t)
                tt("dve", dst, tpv, gain_ap.unsqueeze(2).to_broadcast([128, 8, 128]), ALU.mult,
                   [r_bank[bk], r_const], [rdst])
            if reuse_rstd:
                return [st_load, st_xn, st_tr, st_evac]
            return [st_load, st_stat, st_xn, st_tr, st_evac]

        nst1 = norm_stages(g_attn, lambda t: (hTt[t % 2], r_hTt[t % 2]), xs, r_xs, xn, r_xn, junk, r_junk,
                           stat, r_stat, [0, 1])

        statB = [A.alloc(64, F32) for _ in range(2)]
        r_statB = [Res(f"statB{i}") for i in range(2)]
        assert A.off <= ROPE_OFF, A.off

        def st_latmm(t):
            h = hTt[t % 2]
            b0 = 2 + 2 * (t % 2)
            b1 = b0 + 1
            for k in range(8):
                mm(banks[b0][:, 0:384], h[:, k, :], Wlat[:, k, 0:384], k == 0, k == 7, [r_hTt[t % 2], r_Wlat], [r_bank[b0]])
            for k in range(8):
                mm(banks[b1][:, 0:384], h[:, k, :], Wlat[:, k, 384:768], k == 0, k == 7, [r_hTt[t % 2], r_Wlat], [r_bank[b1]])

        def st_latel(t):
            b0 = 2 + 2 * (t % 2)
            b1 = b0 + 1
            sa = statB[t % 2]
            ln = latn[t % 2]
            rs = [r_statB[t % 2]]
            act(junk[:, 0:384], banks[b0][:, 0:384], ACT.Square, [r_bank[b0]], [r_junk] + rs, accum_out=sa[:, 4:5])
            act(junk[:, 0:256], banks[b1][:, 0:256], ACT.Square, [r_bank[b1]], [r_junk] + rs, accum_out=sa[:, 5:6])
            rt = rtmp[t % 2]
            tt("dve", rt[:, 0:64], banks[b1][:, 256:320], C2[:, t, :], ALU.mult, [r_bank[b1], r_C2], [r_rtmp[t % 2]])
            tt("dve", rt[:, 64:128], banks[b1][:, 320:384], S2[:, t, :], ALU.mult, [r_bank[b1], r_S2], [r_rtmp[t % 2]])
            tt("dve", ln[:, 640:704], rt[:, 0:64], rt[:, 64:128], ALU.add, [r_rtmp[t % 2]], [r_latn[t % 2]])
            tt("dve", ln[:, 704:768], rt[:, 0:64], rt[:, 64:128], ALU.add, [r_rtmp[t % 2]], [r_latn[t % 2]])
            rsqrt_act(sa[:, 8:9], sa[:, 4:5], 384.0, rs, rs, sa[:, 6:7], r_statB[t % 2])
            rsqrt_act(sa[:, 9:10], sa[:, 5:6], 256.0, rs, rs, sa[:, 7:8], r_statB[t % 2])
            ts("dve", ln[:, 0:384], banks[b0][:, 0:384], sa[:, 8:9], None, ALU.mult, None, [r_bank[b0]] + rs, [r_latn[t % 2]])
            act(ln[:, 384:640], banks[b1][:, 0:256], ACT.Copy, [r_bank[b1]] + rs, [r_latn[t % 2]], scale=sa[:, 9:10])

        def tp2v():
            return bank_bf(6).rearrange("p (a b) -> p a b", a=8, b=128)

        def st_tr2(t):
            ln = latn[t % 2]
            tpv = tp2v()
            for c in range(6):
                tr(tpv[:, c, :], ln[:, c * 128:(c + 1) * 128], [r_latn[t % 2]], [r_bank[6]])

        def st_ev2(t):
            tpv = tp2v()
            j = t // 4
            cols = slice(t * 128, (t + 1) * 128)
            tt("dve", c_qT[:, :, cols], tpv[:, 0:3, :], g_q.unsqueeze(2).to_broadcast([128, 3, 128]), ALU.mult,
               [r_bank[6], r_const], [r_cq[j]])
            tt("dve", c_kvT[:, :, cols], tpv[:, 3:5, :], g_kv.unsqueeze(2).to_broadcast([128, 2, 128]), ALU.mult,
               [r_bank[6], r_const], [r_ckv[j]])
            cp("act", k_ropeT[:, cols], tpv[:, 5, :], [r_bank[6]], [r_krope[j]])

        def st_qpe(t):
            j = t // 4
            cols = slice(t * 128, (t + 1) * 128)
            for k in range(3):
                mm(banks[7][:, :], c_qT[:, k, cols], Wqr[:, k, :], k == 0, k == 2, [r_cq[j], r_Wqr], [r_bank[7]])

        def st_qrope(t):
            qt = qtmp[t % 2]
            a3 = qt[:, 0:256].rearrange("p (a b) -> p a b", a=4, b=64)
            b3 = qt[:, 256:512].rearrange("p (a b) -> p a b", a=4, b=64)
            p3 = banks[7][:, 0:256].rearrange("p (a b) -> p a b", a=4, b=64)
            s3 = banks[7][:, 256:512].rearrange("p (a b) -> p a b", a=4, b=64)
            tt("dve", a3, p3, C2[:, t, :].unsqueeze(1).to_broadcast([128, 4, 64]), ALU.mult, [r_bank[7], r_C2], [r_qtmp[t % 2]])
            tt("dve", b3, s3, S2[:, t, :].unsqueeze(1).to_broadcast([128, 4, 64]), ALU.mult, [r_bank[7], r_S2], [r_qtmp[t % 2]])
            tt("dve", qper[t % 2], qt[:, 0:256], qt[:, 256:512], ALU.add, [r_qtmp[t % 2]], [r_qper[t % 2]])

        def st_tr3(t):
            tpv = tp2v()
            for c in range(2):
                tr(tpv[:, 6 + c, :], qper[t % 2][:, c * 128:(c + 1) * 128], [r_qper[t % 2]], [r_bank[6]])

        def st_cp3(t):
            j = t // 4
            cols = slice(t * 128, (t + 1) * 128)
            tpv = tp2v()
            cp("act", q_ropeT[:, :, cols], tpv[:, 6:8, :], [r_bank[6]], [r_qrope[j]])

        skew(NT, nst1 + [st_latmm, st_latel, st_tr2, st_ev2, st_qpe, st_qrope, st_tr3, st_cp3])
        dbg_dump("cqT", c_qT.rearrange("p a b -> p (a b)"), 3 * S_LEN, BF16)
        dbg_dump("ckvT", c_kvT.rearrange("p a b -> p (a b)"), 2 * S_LEN, BF16)
        dbg_dump("kropeT", k_ropeT, S_LEN, BF16)
        dbg_dump("qropeT", q_ropeT.rearrange("p a b -> p (a b)"), 2 * S_LEN, BF16)
        dbg_dump("C2", C2.rearrange("p a b -> p (a b)"), 2048, F32)
        dbg_dump("S2", S2.rearrange("p a b -> p (a b)"), 2048, F32)

        S.barrier()
        A.reset(BOUT_END)
        q_nT = A.alloc(S_LEN * 2, BF16)
        k_nT = A.alloc(S_LEN * 2, BF16)
        Vh = A.alloc(S_LEN * 2, BF16, (32, 128))
        r_qn = [Res(f"qn{j}") for j in range(NQ)]
        r_kn = [Res(f"kn{j}") for j in range(NQ)]
        r_V = [Res(f"V{j}") for j in range(NQ)]
        Wuqn = A.alloc(3 * 512 * 2, BF16, (3, 512))
        Wukn = A.alloc(2 * 512 * 2, BF16, (2, 512))
        Wukv = A.alloc(2 * 512 * 2, BF16, (2, 512))
        r_Wm = Res("Wmla")
        PT = [A.alloc(1024, BF16) for _ in range(8)]
        r_PT = [Res(f"PT{i}") for i in range(8)]
        Rt = [A.alloc(2048, F32) for _ in range(2)]
        r_Rt = [Res(f"Rt{i}") for i in range(2)]
        Dsb = [A.alloc(2048, F32) for _ in range(2)]
        r_Dsb = [Res(f"Dsb{i}") for i in range(2)]
        kr_hi = A.alloc(S_LEN * 2, BF16)
        kr_lo = k_ropeT
        S.op("dve", lambda e: e.memset(kr_hi[0:64, :], 0.0), [], r_krope)
        S.op("dve", lambda e: e.tensor_copy(out=kr_hi[64:128, :], in_=k_ropeT[64:128, :]), r_krope, r_krope)
        S.op("dve", lambda e: e.memset(k_ropeT[64:128, :], 0.0), r_krope, r_krope)
        kr_pad = [kr_lo, kr_hi]
        for h in range(4):
            dma("pool", Wuqn[:, :, h * 128:(h + 1) * 128], wview(w_uq_d, h * 192, h * 192 + 128), [], [r_Wm])
            dma("pool", Wukn[:, :, h * 128:(h + 1) * 128], wview(w_ukv_d, h * 256, h * 256 + 128), [], [r_Wm])
            dma("pool", Wukv[:, :, h * 128:(h + 1) * 128], wview(w_ukv_d, h * 256 + 128, h * 256 + 256), [], [r_Wm])

        r_w1b = [Res(f"w1b{i}") for i in range(8)]
        r_w2b = [Res(f"w2b{i}") for i in range(8)]
        r_wob = [Res(f"wob{i}") for i in range(2)]
        for i in range(2):
            dma("pool", wob_d[i * 512:(i + 1) * 512, :], w_out_d[i * 512:(i + 1) * 512, :], [], [r_wob[i]])
        for i in range(8):
            dma("pool", w1b_d[i * 128:(i + 1) * 128, :], w1_d[i * 128:(i + 1) * 128, :], [], [r_w1b[i]])
        for i in range(8):
            dma("pool", w2b_d[i * 512:(i + 1) * 512, :], w2_d[i * 512:(i + 1) * 512, :], [], [r_w2b[i]])

        proj_ring = [0, 1, 2, 7]
        pr_ctr = [0]

        def next_pb():
            return next_sb()

        fin_ctr = [0]
        MLA_SCALE = 192.0 ** -0.5
        for h in range(4):
            def mla_tasks(j, h=h):
                cols = slice(j * 512, (j + 1) * 512)

                def t_q():
                    b = next_pb()
                    for k in range(3):
                        mm(banks[b][:, :], Wuqn[:, k, h * 128:(h + 1) * 128], c_qT[:, k, cols], k == 0, k == 2,
                           [r_Wm, r_cq[j]], [r_bank[b]])
                    cp("dve", q_nT[:, cols], banks[b][:, :], [r_bank[b]], [r_qn[j]])

                def t_k():
                    b = next_pb()
                    for k in range(2):
                        mm(banks[b][:, :], Wukn[:, k, h * 128:(h + 1) * 128], c_kvT[:, k, cols], k == 0, k == 1,
                           [r_Wm, r_ckv[j]], [r_bank[b]])
                    cp("dve", k_nT[:, cols], banks[b][:, :], [r_bank[b]], [r_kn[j]])

                def t_v():
                    b = next_pb()
                    bv = banks[b][:, :].rearrange("p (a b) -> p a b", a=4, b=128)
                    for tt_ in range(4):
                        t = 4 * j + tt_
                        for k in range(2):
                            mm(bv[:, tt_, :], c_kvT[:, k, t * 128:(t + 1) * 128], Wukv[:, k, h * 128:(h + 1) * 128],
                               k == 0, k == 1, [r_Wm, r_ckv[j]], [r_bank[b]])
                    cp("dve", Vh[:, 4 * j:4 * j + 4, :], bv, [r_bank[b]], [r_V[j]])
                return [t_q, t_k, t_v]

            for fn in mla_tasks(0):
                fn()

            half = (h % 2) * 64
            pair = h // 2

            def fin_mla(u, ob, db, later, h=h):
                j = u["j"]
                cols = slice(j * 512, (j + 1) * 512)
                f = fin_ctr[0] % 2
                fin_ctr[0] += 1
                cp("dve", Dsb[f], banks[db][:, :], [r_bank[db]], [r_Dsb[f]])

                def tail(f=f, j=j, cols=cols, ob=ob):
                    mb = next_sb()
                    mm(banks[mb][:, :], c32[:, :], Dsb[f], True, True, [r_const, r_Dsb[f]], [r_bank[mb]])
                    act(Rt[f], banks[mb][:, :], ACT.Ln, [r_bank[mb]], [r_Rt[f]])
                    act(Rt[f], Rt[f], ACT.Exp, [r_Rt[f]], [r_Rt[f]], scale=-1.0)
                    tt("dve", mixT[:, 4 + h, cols], banks[ob][:, :], Rt[f], ALU.mult, [r_bank[ob], r_Rt[f]], [r_mix[4 + h][j]])
                later(tail, 2)

            units = []
            for j in range(NQ):
                cols = slice(j * 512, (j + 1) * 512)
                units.append(dict(
                    j=j, scale=MLA_SCALE, bias=None, strip=4, nearmin=0,
                    kparts=[
                        (lambda c: (k_nT[:, c * 128:(c + 1) * 128], r_kn[c // 4]), q_nT[:, cols], r_qn[j]),
                        (lambda c, hh=h % 2: (kr_pad[hh][:, c * 128:(c + 1) * 128], r_krope[c // 4]),
                         q_ropeT[:, pair, cols], r_qrope[j]),
                    ],
                    v=lambda c: (Vh[:, c, :], r_V[c // 4]),
                    tasks=(mla_tasks(j + 1) if j + 1 < NQ else []),
                    fin=fin_mla))
            run_attention(units, PT, r_PT)
            if h == 0:
                dbg_dump("qnT", q_nT, S_LEN, BF16)
                dbg_dump("knT", k_nT, S_LEN, BF16)
                dbg_dump("Vh", Vh.rearrange("p a b -> p (a b)"), S_LEN, BF16)

        S.barrier()
        A.reset()
        hT = A.alloc(8 * S_LEN * 2, BF16, (8, S_LEN))
        r_hT = [Res(f"hT{j}") for j in range(NQ)]
        HT_END = A.off
        xs = [A.alloc(4096, F32) for _ in range(6)]
        r_xs = [Res(f"xsb{i}") for i in range(6)]
        junk = A.alloc(2048, BF16)
        r_junk = Res("junkb")
        xn = [A.alloc(2048, BF16) for _ in range(2)]
        r_xn = [Res(f"xnb{i}") for i in range(2)]
        stat = [A.alloc(64, F32) for _ in range(2)]
        r_stat = [Res(f"statb{i}") for i in range(2)]
        nst2 = norm_stages(g_attn, lambda t: (hT[:, :, t * 128:(t + 1) * 128], r_hT[t // 4]), xs, r_xs, xn, r_xn,
                           junk, r_junk, stat, r_stat, [0, 1], reuse_rstd=True)
        skew(NT, nst2)

        S.barrier()
        A.reset(HT_END)
        qT = A.alloc(S_LEN * 2, BF16)
        kT0 = A.alloc(S_LEN * 2, BF16)
        kT1 = A.alloc(S_LEN * 2, BF16)
        kTs = [kT0, kT1]
        Vd = A.alloc(S_LEN * 2, BF16, (32, 128))
        r_q = [Res(f"q{j}") for j in range(NQ)]
        r_k = [Res(f"k{j}") for j in range(NQ)]
        r_Vd = [Res(f"Vd{j}") for j in range(NQ)]
        Wh = A.alloc(8 * 384 * 2, BF16, (8, 384))
        r_Wh = Res("Wh")
        PT = [A.alloc(1024, BF16) for _ in range(8)]
        r_PT = [Res(f"PTd{i}") for i in range(8)]
        On = [A.alloc(2048, F32) for _ in range(3)]
        r_On = [Res(f"On{i}") for i in range(3)]
        Dsb = A.alloc(2048, F32)
        r_Dsbd = Res("Dsbd")
        Ct = [A.alloc(2048, F32) for _ in range(2)]
        r_Ct = [Res(f"Ct{i}") for i in range(2)]
        sqt = [A.alloc(1024, BF16) for _ in range(2)]
        r_sqt = [Res(f"sqt{i}") for i in range(2)]
        lnt = [A.alloc(2048, F32) for _ in range(2)]
        r_lnt = [Res(f"lnt{i}") for i in range(2)]
        S.op("pool", lambda e: e.memset(kT0[64:128, :], 0.0), [], r_k)
        S.op("pool", lambda e: e.memset(kT1[0:64, :], 0.0), [], r_k)

        def load_Wh(h):
            for part in range(3):
                c0 = part * 512 + h * 128
                dma("pool", Wh[:, :, part * 128:(part + 1) * 128], wview(w_in_d, c0, c0 + 128), [], [r_Wh])

        load_Wh(0)
        pair_ctr = [0]
        DIFF_SCALE = 64.0 ** -0.5
        for h in range(4):
            def diff_tasks(j, h=h):
                w = Wh
                rw = r_Wh
                cols = slice(j * 512, (j + 1) * 512)

                def t_q():
                    b = next_pb()
                    for k in range(8):
                        mm(banks[b][:, :], w[:, k, 0:128], hT[:, k, cols], k == 0, k == 7, [rw, r_hT[j]], [r_bank[b]])
                    cp("dve", qT[:, cols], banks[b][:, :], [r_bank[b]], [r_q[j]])

                def t_k():
                    b = next_pb()
                    for k in range(8):
                        mm(banks[b][:, :], w[:, k, 128:256], hT[:, k, cols], k == 0, k == 7, [rw, r_hT[j]], [r_bank[b]])
                    cp("dve", kT0[0:64, cols], banks[b][0:64, :], [r_bank[b]], [r_k[j]])
                    cp("dve", kT1[64:128, cols], banks[b][64:128, :], [r_bank[b]], [r_k[j]])

                def t_v(lo_, hi_):
                    def f():
                        b = next_pb()
                        bv = banks[b][:, :].rearrange("p (a b) -> p a b", a=4, b=128)
                        for tt_ in range(lo_, hi_):
                            t = 4 * j + tt_
                            for k in range(8):
                                mm(bv[:, tt_, :], hT[:, k, t * 128:(t + 1) * 128], w[:, k, 256:384], k == 0, k == 7,
                                   [rw, r_hT[j]], [r_bank[b]])
                        cp("dve", Vd[:, 4 * j + lo_:4 * j + hi_, :], bv[:, lo_:hi_, :], [r_bank[b]], [r_Vd[j]])
                    return f
                return [t_q, t_k, t_v(0, 2), t_v(2, 4)]

            for fn in diff_tasks(0):
                fn()

            def fin_diff(u, ob, db, later, h=h):
                j = u["j"]
                comp = u["comp"]
                cols = slice(j * 512, (j + 1) * 512)
                pi = pair_ctr[0] % 2
                oi = pi if comp == 0 else 2
                if comp == 1:
                    pair_ctr[0] += 1
                cp("dve", Dsb, banks[db][:, :], [r_bank[db]], [r_Dsbd])

                def tail(pi=pi, oi=oi, comp=comp, j=j, cols=cols, ob=ob):
                    mb = next_sb()
                    mm(banks[mb][:, :], c32[:, :], Dsb, True, True, [r_const, r_Dsbd], [r_bank[mb]])
                    act(On[oi], banks[mb][:, :], ACT.Ln, [r_bank[mb]], [r_On[oi]])
                    act(On[oi], On[oi], ACT.Exp, [r_On[oi]], [r_On[oi]], scale=-1.0)
                    tt("dve", On[oi], banks[ob][:, :], On[oi], ALU.mult, [r_bank[ob], r_On[oi]], [r_On[oi]])
                    if comp == 1:
                        C = Ct[pi]
                        stt("dve", C, On[2], small[:, 4:5], On[pi], ALU.mult, ALU.add,
                            [r_On[2], r_On[pi], r_small], [r_Ct[pi]])
                        tt("dve", sqt[pi], C, C, ALU.mult, [r_Ct[pi]], [r_sqt[pi]])

                        def tail2(pi=pi, j=j, cols=cols):
                            mb = next_sb()
                            mm(banks[mb][:, :], ones[:, :], sqt[pi], True, True, [r_const, r_sqt[pi]], [r_bank[mb]])
                            act(lnt[pi], banks[mb][:, :], ACT.Ln, [r_bank[mb]], [r_lnt[pi]], scale=1.0 / 128.0, bias=EPS)
                            act(lnt[pi], lnt[pi], ACT.Exp, [r_lnt[pi]], [r_lnt[pi]], scale=-0.5)
                            stt("dve", mixT[:, h, cols], Ct[pi], small[:, 5:6], lnt[pi], ALU.mult, ALU.mult,
                                [r_Ct[pi], r_small, r_lnt[pi]], [r_mix[h][j]])
                        later(tail2, 6)
                later(tail, 2)

            units = []
            for j in range(NQ):
                cols = slice(j * 512, (j + 1) * 512)
                for comp in range(2):
                    units.append(dict(
                        j=j, comp=comp, scale=DIFF_SCALE, bias=small[:, h:h + 1], strip=h, nearmin=-1,
                        kparts=[(lambda c, comp=comp: (kTs[comp][:, c * 128:(c + 1) * 128], r_k[c // 4]),
                                 qT[:, cols], r_q[j])],
                        v=lambda c: (Vd[:, c, :], r_Vd[c // 4]),
                        tasks=(((diff_tasks(j + 1)[2 * comp:2 * comp + 2]) if j + 1 < NQ else [])
                               + ([lambda h=h: load_Wh(h + 1)] if (j + 2 == NQ and comp == 1 and h + 1 < 4) else [])),
                        fin=fin_diff))
            run_attention(units, PT, r_PT)

        if dump_mix:
            S.barrier()
            dma("sp", mixdump_d[:, :], mixT[:, :, :].rearrange("p a b -> p (a b)"), [r for rr in r_mix for r in rr], [],
                tag="out")

        A.reset()
        Wo = A.alloc(8 * 1024 * 2, BF16, (8, 1024))
        r_Wo = Res("Wo")
        xb = [A.alloc(4096, F32) for _ in range(8)]
        r_xb = [Res(f"xb{i}") for i in range(8)]
        xn = [A.alloc(2048, BF16) for _ in range(2)]
        r_xn = [Res(f"xnc{i}") for i in range(2)]
        junk = A.alloc(2048, BF16)
        r_junk = Res("junkc")
        h2T = A.alloc(8 * 512 * 2, BF16, (8, 512))
        r_h2T = [Res(f"h2T{s}") for s in range(4)]
        uT = A.alloc(32 * 512 * 2, BF16, (32, 512))
        r_uT = [Res(f"uT{c}") for c in range(32)]
        rl = [A.alloc(2048, F32) for _ in range(2)]
        r_rl = [Res("rl0"), Res("rl1")]
        W1s = [A.alloc(8 * 512 * 2, BF16, (8, 512)) for _ in range(2)]
        r_W1s = [Res("W1s0"), Res("W1s1")]
        W2s = [A.alloc(4 * 512 * 2, BF16, (4, 512)) for _ in range(2)]
        r_W2s = [Res("W2s0"), Res("W2s1")]
        gfin = A.alloc(4096, F32)
        r_gfin = Res("gfin")
        stat = [A.alloc(64, F32) for _ in range(2)]
        r_stat = [Res(f"statc{i}") for i in range(2)]

        dma("sp", Wo, wview(wob_d, 0, 1024), r_wob, [r_Wo] + r_hT)
        for s_ in range(4):
            dma("pool", xb[s_], x_d[s_ * 128:(s_ + 1) * 128, :], [], [r_xb[s_]] + r_hT)
        S.barrier()
        dma("sp", gfin, bass.AP(gfin_d, 0, [[0, 128], [1, 1024]]), [], [r_gfin])

        w1_ctr = [0]
        w2_ctr = [0]
        ffn_ring = [0]
        st_ctr = [0]

        def op_load(i):
            for s_ in range(4):
                t = 4 * i + s_
                bi = (4 * i + s_) % 8
                dma("pool", xb[bi], x_d[t * 128:(t + 1) * 128, :], [], [r_xb[bi]])

        def op_a(i, s_):
            t = 4 * i + s_
            bi = (4 * i + s_) % 8
            buf = xb[bi]
            for half in range(2):
                b = 4 + half
                for k in range(8):
                    mm(banks[b][:, :], mixT[:, k, t * 128:(t + 1) * 128], Wo[:, k, half * 512:(half + 1) * 512],
                       k == 0, k == 7, [r_mix[k][i], r_Wo], [r_bank[b]])
                hs = slice(half * 512, (half + 1) * 512)
                tt("dve", buf[:, hs], buf[:, hs], banks[b][:, :], ALU.add, [r_xb[bi], r_bank[b]], [r_xb[bi]])

        def op_b(i, s_):
            bi = (4 * i + s_) % 8
            buf = xb[bi]
            si = st_ctr[0] % 2
            st_ctr[0] += 1
            sa = stat[si]
            rs = [r_stat[si]]
            act(junk, buf, ACT.Square, [r_xb[bi]], [r_junk] + rs, accum_out=sa[:, 0:1])
            rsqrt_act(sa[:, 2:3], sa[:, 0:1], 1024.0, rs, rs, sa[:, 1:2], r_stat[si])
            ts("dve", xn[s_ % 2], buf, sa[:, 2:3], None, ALU.mult, None, [r_xb[bi]] + rs, [r_xn[s_ % 2]])

        def op_c(i, s_):
            tpv = bank_bf(6).rearrange("p (a b) -> p a b", a=8, b=128)
            for c in range(8):
                tr(tpv[:, c, :], xn[s_ % 2][:, c * 128:(c + 1) * 128], [r_xn[s_ % 2]], [r_bank[6]])
            tt("dve", h2T[:, :, s_ * 128:(s_ + 1) * 128], tpv, g_mlp.unsqueeze(2).to_broadcast([128, 8, 128]), ALU.mult,
               [r_bank[6], r_const], [r_h2T[s_]])

        OP_SLOTS = {0: [("a", 0)], 1: [("b", 0), ("a", 1)], 2: [("c", 0), ("b", 1), ("a", 2)],
                    3: [("c", 1), ("b", 2), ("a", 3)], 4: [("c", 2), ("b", 3)], 5: [("c", 3)]}

        def op_slot(i, slot):
            for kind, s_ in OP_SLOTS.get(slot, []):
                {"a": op_a, "b": op_b, "c": op_c}[kind](i, s_)

        def FFN1(i):
            if i + 1 < NQ:
                op_load(i + 1)
            for g in range(8):
                wi = w1_ctr[0] % 2
                w1_ctr[0] += 1
                dma("sp", W1s[wi], wview(w1b_d, g * 512, (g + 1) * 512), r_w1b, [r_W1s[wi]])
                for cc in range(4):
                    c = 4 * g + cc
                    b = ffn_ring[0] % 8
                    ffn_ring[0] += 1
                    for k in range(8):
                        mm(banks[b][:, :], W1s[wi][:, k, cc * 128:(cc + 1) * 128], h2T[:, k, :], k == 0, k == 7,
                           [r_W1s[wi]] + r_h2T, [r_bank[b]])
                    ri = c % 2
                    act(rl[ri], banks[b][:, :], ACT.Relu, [r_bank[b]], [r_rl[ri]])
                    tt("dve", uT[:, c, :], rl[ri], rl[ri], ALU.mult, [r_rl[ri]], [r_uT[c]])

        def FFN2(i):
            for half in range(2):
                hs = slice(half * 512, (half + 1) * 512)
                for g in range(8):
                    wi = w2_ctr[0] % 2
                    w2_ctr[0] += 1
                    dma("sp", W2s[wi], w2b_d[g * 512:(g + 1) * 512, hs].rearrange("(k p) c -> p k c", p=128), [r_w2b[g]],
                        [r_W2s[wi]])
                    for s in range(4):
                        b = 4 * half + s
                        for cc in range(4):
                            c = 4 * g + cc
                            mm(banks[b][:, :], uT[:, c, s * 128:(s + 1) * 128], W2s[wi][:, cc, :], c == 0, c == 31,
                               [r_uT[c], r_W2s[wi]], [r_bank[b]])
                    if half == 0 and i + 1 < NQ:
                        op_slot(i + 1, g)
                for s in range(4):
                    b = 4 * half + s
                    bi = (4 * i + s) % 8
                    tt("dve", xb[bi][:, hs], xb[bi][:, hs], banks[b][:, :], ALU.add, [r_xb[bi], r_bank[b]], [r_xb[bi]])
            for s in range(4):
                t = 4 * i + s
                bi = (4 * i + s) % 8
                buf = xb[bi]
                si = st_ctr[0] % 2
                st_ctr[0] += 1
                sa = stat[si]
                rs = [r_stat[si]]
                act(junk, buf, ACT.Square, [r_xb[bi]], [r_junk] + rs, accum_out=sa[:, 0:1])
                rsqrt_act(sa[:, 2:3], sa[:, 0:1], 1024.0, rs, rs, sa[:, 1:2], r_stat[si])
                stt("dve", buf, buf, sa[:, 2:3], gfin, ALU.mult, ALU.mult, [r_xb[bi], r_gfin] + rs, [r_xb[bi]])
                dma("pool", y_d[t * 128:(t + 1) * 128, :], buf, [r_xb[bi]], [], tag="out")

        for slot in range(6):
            op_slot(0, slot)
        for i in range(NQ):
            FFN1(i)
            FFN2(i)

        S.finalize()
        csems = {e: es.enter_context(nc.semaphore("c_" + e)) for e in ("pe", "act", "dve", "pool")}
        dsems = {}
        for q, n in S.nslots.items():
            for i in range(n):
                dsems[(q, i)] = es.enter_context(nc.semaphore(f"d_{q}_{i}"))
        block = es.enter_context(nc.Block())

        @block.tensor
        def _(e):
            S.emit_engine("pe", e, csems, dsems)

        @block.scalar
        def _(e):
            S.emit_engine("act", e, csems, dsems)

        @block.vector
        def _(e):
            S.emit_engine("dve", e, csems, dsems)

        @block.gpsimd
        def _(e):
            S.emit_engine("pool", e, csems, dsems)

        @block.sync
        def _(e):
            S.emit_engine("sp", e, csems, dsems)

    return nc, S


def _col_layout(v, nch):
    return np.ascontiguousarray(np.asarray(v, np.float32).reshape(nch, 128).T)


def make_in_maps(inputs):
    f = lambda a: np.ascontiguousarray(np.asarray(a, np.float32))
    x = f(inputs["x"])
    gains = np.concatenate([
        _col_layout(inputs["norm_attn"][0], 8), _col_layout(inputs["norm_mlp"][0], 8),
        _col_layout(inputs["mla_q_norm"][0], 3), _col_layout(inputs["mla_kv_norm"][0], 2),
        _col_layout(inputs["diff_subln"][0], 1)], axis=1)
    lamv = np.concatenate([f(inputs["diff_lq1"][0]), f(inputs["diff_lq2"][0]),
                           f(inputs["diff_lk1"][0]), f(inputs["diff_lk2"][0])])[None, :]
    pos = np.ascontiguousarray(np.asarray(inputs["positions"], np.int32).reshape(NT, 128).T)
    invf = (np.float32(10000.0) ** (-np.arange(0, 64, 2, dtype=np.float32) / np.float32(64)))[None, :].astype(np.float32)
    shared = {
        "w_in": f(inputs["w_in"][0]), "w_uq": f(inputs["mla_w_uq"][0]), "w_ukv": f(inputs["mla_w_ukv"][0]),
        "w_out": f(inputs["w_out"][0]), "w1": f(inputs["w_mlp_in"][0]), "w2": f(inputs["w_mlp_out"][0]),
        "gains": np.ascontiguousarray(gains), "gfin": f(inputs["norm_final"])[None, :],
        "relb": f(inputs["rel_bias"]), "lamv": np.ascontiguousarray(lamv), "pos": pos, "invf": invf,
        "oh": _onehot_table(), "ident": np.eye(128, dtype=np.float32).astype(ml_dtypes.bfloat16),
        "antiid": np.ascontiguousarray(np.eye(128, dtype=np.float32)[::-1]).astype(ml_dtypes.bfloat16),
    }
    return [dict(shared, x=np.ascontiguousarray(x[b])) for b in range(x.shape[0])]


def kernel(**inputs):
    in_maps = make_in_maps(inputs)
    nc, _ = build_program()
    res = run_bass_kernel_spmd(nc, in_maps, core_ids=list(range(8)))
    return np.stack([np.asarray(r["y"], np.float32) for r in res.results], axis=0)
```

```python
import math
import os
from contextlib import ExitStack

import numpy as np
import ml_dtypes

import concourse.bass as bass
import concourse.mybir as mybir
from concourse.bass_utils import run_bass_kernel_spmd

F32 = mybir.dt.float32
BF16 = mybir.dt.bfloat16
I32 = mybir.dt.int32
ACT = mybir.ActivationFunctionType
ALU = mybir.AluOpType
AX = mybir.AxisListType

S_LEN = 4096
D_MODEL = 1024
NT = S_LEN // 128
NQ = S_LEN // 512
EPS = 1e-6
TAB_L = 1151


class Res:
    __slots__ = ("name", "psum", "w", "r", "rd")

    def __init__(self, name, psum=False):
        self.name = name
        self.psum = psum
        self.w = None
        self.r = {}
        self.rd = []


class Op:
    __slots__ = ("eng", "fn", "idx", "deps_c", "deps_d", "signal", "is_dma",
                 "dsem", "dval", "prevval", "sval", "tag")

    def __init__(self, eng, fn, is_dma, tag=None):
        self.eng = eng
        self.fn = fn
        self.is_dma = is_dma
        self.deps_c = {}
        self.deps_d = []
        self.signal = False
        self.dsem = None
        self.dval = 0
        self.prevval = 0
        self.sval = 0
        self.idx = -1
        self.tag = tag


class Sched:
    ENGS = ("pe", "act", "dve", "pool", "sp")

    def __init__(self, ndma_slots=None):
        self.streams = {e: [] for e in self.ENGS}
        self.nslots = ndma_slots or {"sp": 12, "pool": 40}
        self.last_c = {}
        self.dmas_since = []
        self.pending = {}

    def _dep(self, op, d, kind):
        if d is None or d is op:
            return
        if (not d.is_dma) and (not op.is_dma) and d.eng == op.eng:
            if op.eng == "pe" or kind == "bar":
                return
        if d.is_dma:
            if d not in op.deps_d:
                op.deps_d.append(d)
        else:
            cur = op.deps_c.get(d.eng)
            if cur is None or d.idx > cur.idx:
                op.deps_c[d.eng] = d

    def _record(self, op, reads, writes):
        writes = list(writes)
        rds = []
        for R in reads:
            if R.psum:
                if R not in writes:
                    writes.append(R)
            else:
                rds.append(R)
        pend = self.pending.pop(op.eng, None)
        if pend is not None:
            for d in pend:
                self._dep(op, d, "bar")
        for R in rds:
            self._dep(op, R.w, "raw")
        for R in writes:
            self._dep(op, R.w, "waw")
            for rd in R.r.values():
                self._dep(op, rd, "war")
            for rd in R.rd:
                self._dep(op, rd, "war")
        for R in rds:
            if R in writes:
                continue
            if op.is_dma:
                R.rd.append(op)
            else:
                R.r[op.eng] = op
        for R in writes:
            R.w = op
            R.r = {}
            R.rd = []
        st = self.streams[op.eng]
        op.idx = len(st)
        st.append(op)
        if op.is_dma:
            self.dmas_since.append(op)
        else:
            self.last_c[op.eng] = op
        return op

    def op(self, eng, fn, reads=(), writes=(), tag=None):
        return self._record(Op(eng, fn, False, tag), reads, writes)

    def dma(self, queue, fn, reads=(), writes=(), tag=None):
        return self._record(Op(queue, fn, True, tag), reads, writes)

    def barrier(self):
        deps = list(self.last_c.values()) + list(self.dmas_since)
        for e in self.ENGS:
            old = self.pending.get(e)
            self.pending[e] = (old or []) + deps if old else list(deps)
        self.dmas_since = []

    def finalize(self):
        for e in self.ENGS:
            for op in self.streams[e]:
                for d in op.deps_c.values():
                    d.signal = True
        self.stats = {}
        for e in self.ENGS:
            cnt = 0
            i = 0
            n = self.nslots.get(e, 1)
            for op in self.streams[e]:
                if op.is_dma:
                    op.dsem = (e, i % n)
                    op.dval = 16 * (i // n + 1)
                    op.prevval = 16 * (i // n)
                    i += 1
                elif op.signal:
                    cnt += 1
                    op.sval = cnt
            self.stats[e] = (len(self.streams[e]), cnt, i)

    def emit_engine(self, e, eng, csems, dsems):
        waited = {}
        nwaits = 0
        for op in self.streams[e]:
            waits = []
            for d in op.deps_c.values():
                waits.append((("c", d.eng), csems[d.eng], d.sval))
            for d in op.deps_d:
                waits.append((("d",) + d.dsem, dsems[d.dsem], d.dval))
            if op.is_dma and op.prevval > 0:
                waits.append((("d",) + op.dsem, dsems[op.dsem], op.prevval))
            need = {}
            for key, sem, val in waits:
                if waited.get(key, 0) < val and need.get(key, (None, 0))[1] < val:
                    need[key] = (sem, val)
            need = list(need.items())
            emb = None
            if need and not op.is_dma:
                emb = need.pop()
            for key, (sem, val) in need:
                eng.wait_ge(sem, val)
                waited[key] = val
                nwaits += 1
            ins = op.fn(eng)
            if emb is not None:
                ins._wait_ge(emb[1][0], emb[1][1])
                waited[emb[0]] = emb[1][1]
            if op.is_dma:
                ins.then_inc(dsems[op.dsem], 16)
            elif op.signal:
                ins.then_inc(csems[e], 1)
        if e == "sp":
            for q in self.ENGS:
                for op in self.streams[q]:
                    if op.is_dma and op.tag == "out":
                        key = ("d",) + op.dsem
                        if waited.get(key, 0) < op.dval:
                            eng.wait_ge(dsems[op.dsem], op.dval)
                            waited[key] = op.dval
        return nwaits


def _t5_bucket_np(n):
    n = np.maximum(n, 0)
    max_exact = 16
    nf = np.maximum(n, 1).astype(np.float32)
    large = max_exact + (np.log(nf / np.float32(max_exact)) / np.float32(math.log(128 / 16))
                         * np.float32(32 - max_exact)).astype(np.int32)
    large = np.minimum(large, 31)
    return np.where(n < max_exact, n, large)


def _onehot_table():
    d = np.arange(TAB_L) - 511
    oh = np.zeros((33, TAB_L), np.float32)
    b = _t5_bucket_np(d)
    for i in range(TAB_L):
        if d[i] >= 0:
            oh[b[i], i] = 1.0
        else:
            oh[32, i] = 1.0
    return oh


def build_program(stop_after=None, dump_mix=False, dumps=()):
    nc = bass.Bass("TRN2", target_bir_lowering=False)
    S = Sched()
    es = ExitStack()

    def dbg_dump(name, ap, ncols, dt):
        if name not in dumps:
            return
        dd = nc.dram_tensor("dbg_" + name, [128, ncols], dt, kind="ExternalOutput")
        S.barrier()
        S.dma("sp", lambda e: e.dma_start(out=dd[:, :], in_=ap), [], [], tag="out")
        S.barrier()

    def din(name, shape, dt):
        return nc.dram_tensor(name, shape, dt, kind="ExternalInput")

    x_d = din("x", [S_LEN, D_MODEL], F32)
    w_in_d = din("w_in", [1024, 2240], F32)
    w_uq_d = din("w_uq", [384, 768], F32)
    w_ukv_d = din("w_ukv", [256, 1024], F32)
    w_out_d = din("w_out", [1024, 1024], F32)
    w1_d = din("w1", [1024, 4096], F32)
    w2_d = din("w2", [4096, 1024], F32)
    gains_d = din("gains", [128, 22], F32)
    gfin_d = din("gfin", [1, 1024], F32)
    relb_d = din("relb", [32, 4], F32)
    lam_d = din("lamv", [1, 256], F32)
    pos_d = din("pos", [128, 32], I32)
    invf_d = din("invf", [1, 32], F32)
    oh_d = din("oh", [33, TAB_L], F32)
    ident_d = din("ident", [128, 128], BF16)
    antiid_d = din("antiid", [128, 128], BF16)
    y_d = nc.dram_tensor("y", [S_LEN, D_MODEL], F32, kind="ExternalOutput")
    scr_d = nc.dram_tensor("scr", [5, TAB_L], BF16, kind="Internal")
    w1b_d = nc.dram_tensor("w1b", [1024, 4096], BF16, kind="Internal")
    w2b_d = nc.dram_tensor("w2b", [4096, 1024], BF16, kind="Internal")
    wob_d = nc.dram_tensor("wob", [1024, 1024], BF16, kind="Internal")
    mixdump_d = None
    if dump_mix:
        mixdump_d = nc.dram_tensor("mixdump", [128, 8 * S_LEN], BF16, kind="ExternalOutput")

    with es:
        def sb(name, shape, dt):
            return es.enter_context(nc.sbuf_tensor(name, shape, dt))

        ident = sb("ident_sb", [128, 128], BF16)
        antiid = sb("antiid_sb", [128, 128], BF16)
        ones = sb("ones_sb", [128, 128], BF16)
        c32 = sb("c32_sb", [128, 128], F32)
        gains = sb("gains_sb", [128, 22], F32)
        small = sb("small_sb", [128, 16], F32)
        rstd_all = sb("rstd_all_sb", [128, 32], F32)
        r_rstd = [Res(f"rstd{t}") for t in range(32)]
        strips = sb("strips_sb", [128, 5, 1024], BF16)
        mixT = sb("mixT_sb", [128, 8, S_LEN], BF16)
        ARENA_BYTES = 128 * 1024
        arena = sb("arena_sb", [128, ARENA_BYTES // 2], BF16)
        r_const = Res("const")
        r_small = Res("small")
        r_strips = Res("strips")
        r_mix = [[Res(f"mix{c}_{j}") for j in range(NQ)] for c in range(8)]

        banks = [es.enter_context(nc.psum_tensor(f"bank{i}", [128, 512], F32)) for i in range(8)]
        r_bank = [Res(f"bank{i}", True) for i in range(8)]

        def bank_bf(i):
            return banks[i][:, :].bitcast(BF16)

        class Arena:
            def __init__(self, base=None, nbytes=None):
                self.off = 0
                self.base = base
                self.nbytes = nbytes

            def reset(self, off=0):
                self.off = off

            def alloc(self, nbytes, dt, shape3=None):
                off = (self.off + 31) // 32 * 32
                self.off = off + nbytes
                if self.base is None:
                    assert self.off <= ARENA_BYTES, f"arena overflow {self.off}"
                    ap = arena[:, off // 2:(off + nbytes) // 2]
                else:
                    assert self.off <= self.nbytes, f"arena overflow {self.off}"
                    ap = self.base[:, off // 2:(off + nbytes) // 2]
                if dt == F32:
                    ap = ap.bitcast(F32)
                elif dt == I32:
                    ap = ap.bitcast(I32)
                if shape3 is not None:
                    ap = ap.rearrange("p (a b) -> p a b", a=shape3[0], b=shape3[1])
                return ap

        A = Arena()

        def mm(out, lhsT, rhs, start, stop, reads, writes):
            S.op("pe", lambda e: e.matmul(out, lhsT=lhsT, rhs=rhs, start=start, stop=stop), reads, writes)

        def tr(out, in_, reads, writes):
            S.op("pe", lambda e: e.transpose(out=out, in_=in_, identity=ident[:, :]), list(reads) + [r_const], writes)

        def act(out, in_, func, reads, writes, **kw):
            S.op("act", lambda e: e.activation(out=out, in_=in_, func=func, **kw), reads, writes)

        def tt(eng, out, in0, in1, op, reads, writes):
            S.op(eng, lambda e: e.tensor_tensor(out=out, in0=in0, in1=in1, op=op), reads, writes)

        def ts(eng, out, in0, s1, s2, op0, op1, reads, writes):
            if s2 is None:
                S.op(eng, lambda e: e.tensor_scalar(out=out, in0=in0, scalar1=s1, scalar2=None, op0=op0), reads, writes)
            else:
                S.op(eng, lambda e: e.tensor_scalar(out=out, in0=in0, scalar1=s1, scalar2=s2, op0=op0, op1=op1), reads, writes)

        def stt(eng, out, in0, scalar, in1, op0, op1, reads, writes):
            S.op(eng, lambda e: e.scalar_tensor_tensor(out=out, in0=in0, scalar=scalar, in1=in1, op0=op0, op1=op1),
                 reads, writes)

        def cp(eng, out, in_, reads, writes):
            if eng == "act":
                act(out, in_, ACT.Copy, reads, writes)
            else:
                S.op(eng, lambda e: e.tensor_copy(out=out, in_=in_), reads, writes)

        def recip(eng, out, in_, reads, writes):
            S.op(eng, lambda e: e.reciprocal(out=out, in_=in_), reads, writes)

        def dma(q, out, in_, reads, writes, tag=None):
            S.dma(q, lambda e: e.dma_start(out=out, in_=in_), reads, writes, tag=tag)

        def rsqrt_act(out, in_, n, reads, writes, tmp, r_tmp):
            act(tmp, in_, ACT.Ln, list(reads), [r_tmp], scale=1.0 / n, bias=EPS)
            act(out, tmp, ACT.Exp, [r_tmp], writes, scale=-0.5)

        A0 = Arena(mixT[:, :, :].rearrange("p a b -> p (a b)"), 8 * S_LEN * 2)
        A_main = A
        A = A0
        oh_sb = A.alloc(TAB_L * 4, F32)
        relb_aug = A.alloc(32, F32)
        lamt = A.alloc(256 * 4, F32)
        prod = A.alloc(128 * 4, F32)
        etab = A.alloc(1152 * 2, BF16)
        hank = [A.alloc(1024 * 2, BF16) for _ in range(2)]
        posi = A.alloc(32 * 4, I32)
        posf = A.alloc(32 * 4, F32)
        invf = A.alloc(32 * 4, F32)
        ang = A.alloc(1024 * 4, F32, (32, 32))
        u_t = A.alloc(1024 * 4, F32, (32, 32))
        k_i = A.alloc(1024 * 4, I32, (32, 32))
        k_f = A.alloc(1024 * 4, F32, (32, 32))
        fr = A.alloc(1024 * 4, F32, (32, 32))
        stage0_end = 0
        A = A_main
        ROPE_OFF = ARENA_BYTES - 2 * 8192
        A.reset(ROPE_OFF)
        C2 = A.alloc(8192, F32, (32, 64))
        S2 = A.alloc(8192, F32, (32, 64))
        assert stage0_end <= ROPE_OFF
        r_oh, r_relb, r_lamt, r_prod, r_etab = Res("oh"), Res("relb"), Res("lamt"), Res("prod"), Res("etab")
        r_hank = [Res("hank0"), Res("hank1")]
        r_pos, r_posf, r_invf, r_ang, r_u, r_ki, r_kf, r_fr = (Res(n) for n in
                                                                ("pos", "posf", "invf", "ang", "u", "ki", "kf", "fr"))
        r_C2, r_S2, r_scr = Res("C2"), Res("S2"), Res("scr")

        dma("sp", ident[:, :], ident_d[:, :], [], [r_const])
        dma("sp", antiid[:, :], antiid_d[:, :], [], [r_const])
        S.op("pool", lambda e: e.memset(ones[:, :], 1.0), [], [r_const])
        S.op("pool", lambda e: e.memset(c32[:, :], 1.0 / 32.0), [], [r_const])
        dma("sp", gains[:, :], gains_d[:, :], [], [r_const])
        dma("sp", oh_sb[0:33, :], oh_d[:, :], [], [r_oh])
        S.op("pool", lambda e: e.memset(relb_aug[0:33, 0:8], 0.0), [], [r_relb])
        S.op("pool", lambda e: e.memset(relb_aug[32:33, 0:8], -30000.0), [], [r_relb])
        dma("sp", relb_aug[0:32, 0:4], relb_d[:, :], [], [r_relb])
        dma("sp", small[:, 0:4], bass.AP(relb_d, 31 * 4, [[0, 128], [1, 4]]), [], [r_small])
        dma("sp", lamt, bass.AP(lam_d, 0, [[0, 128], [1, 256]]), [], [r_lamt])
        dma("sp", posi, pos_d[:, :], [], [r_pos])
        dma("sp", invf, bass.AP(invf_d, 0, [[0, 128], [1, 32]]), [], [r_invf])

        tt("dve", prod, lamt[:, 0:128], lamt[:, 128:256], ALU.mult, [r_lamt], [r_prod])
        S.op("dve", lambda e: e.tensor_reduce(out=small[:, 8:10], in_=prod.rearrange("p (a b) -> p a b", a=2, b=64),
                                              axis=AX.X, op=ALU.add), [r_prod], [r_small])
        act(small[:, 6:8], small[:, 8:10], ACT.Exp, [r_small], [r_small])
        tt("dve", small[:, 4:5], small[:, 7:8], small[:, 6:7], ALU.subtract, [r_small], [r_small])
        ts("dve", small[:, 4:5], small[:, 4:5], -0.2, None, ALU.add, None, [r_small], [r_small])
        ts("dve", small[:, 5:6], gains[:, 21:22], 0.8, None, ALU.mult, None, [r_const, r_small], [r_small])

        for ci, (c0, c1) in enumerate(((0, 512), (512, 1024), (1024, TAB_L))):
            mm(banks[ci][0:5, 0:c1 - c0], relb_aug[0:33, 0:5], oh_sb[0:33, c0:c1], True, True,
               [r_relb, r_oh], [r_bank[ci]])
            act(etab[0:5, c0:c1], banks[ci][0:5, 0:c1 - c0], ACT.Exp, [r_bank[ci]], [r_etab])
        dma("sp", scr_d[:, :], etab[0:5, 0:TAB_L], [r_etab], [r_scr])
        for h in range(5):
            hk = hank[h % 2]
            dma("sp", hk, bass.AP(scr_d, h * TAB_L, [[1, 128], [1, 1024]]), [r_scr], [r_hank[h % 2]])
            for half in range(2):
                bi = 3 + (2 * h + half) % 4
                mm(banks[bi][:, :], antiid[:, :], hk[:, half * 512:(half + 1) * 512], True, True,
                   [r_const, r_hank[h % 2]], [r_bank[bi]])
                cp("act" if half else "dve", strips[:, h, half * 512:(half + 1) * 512], banks[bi][:, :],
                   [r_bank[bi]], [r_strips])

        cp("dve", posf, posi, [r_pos], [r_posf])
        tt("dve", ang, invf.unsqueeze(1).to_broadcast([128, 32, 32]), posf.unsqueeze(2).to_broadcast([128, 32, 32]),
           ALU.mult, [r_invf, r_posf], [r_ang])
        TWO_PI_S = 6.283184
        for kind in ("sin", "cos"):
            if kind == "sin":
                ts("dve", u_t, ang, 1.0 / (2 * math.pi), None, ALU.mult, None, [r_ang], [r_u])
            else:
                ts("dve", u_t, ang, 1.0 / (2 * math.pi), 0.25, ALU.mult, ALU.add, [r_ang], [r_u])
            cp("dve", k_i, u_t, [r_u], [r_ki])
            cp("dve", k_f, k_i, [r_ki], [r_kf])
            tt("dve", fr, u_t, k_f, ALU.subtract, [r_u, r_kf], [r_fr])
            if kind == "sin":
                act(S2[:, :, 32:64], fr, ACT.Sin, [r_fr], [r_S2], scale=TWO_PI_S)
                act(S2[:, :, 0:32], fr, ACT.Sin, [r_fr], [r_S2], scale=-TWO_PI_S)
            else:
                act(C2[:, :, 0:32], fr, ACT.Sin, [r_fr], [r_C2], scale=TWO_PI_S)
                act(C2[:, :, 32:64], fr, ACT.Sin, [r_fr], [r_C2], scale=TWO_PI_S)

        g_attn = gains[:, 0:8]
        g_mlp = gains[:, 8:16]
        g_q = gains[:, 16:19]
        g_kv = gains[:, 19:21]

        STRIP_ENG = "dve"

        SRING = [0, 1, 2, 7]
        sb_ctr = [0]

        def next_sb():
            b = SRING[sb_ctr[0] % 4]
            sb_ctr[0] += 1
            return b

        def skew(nt, stages):
            ns = len(stages)
            for it in range(nt + ns - 1):
                for s in range(ns - 1, -1, -1):
                    t = it - s
                    if 0 <= t < nt:
                        stages[s](t)

        def run_attention(units, PT, r_PT):
            steps = [(ui, c) for ui, u in enumerate(units) for c in range(4 * u["j"] + 4)]
            DEPTH = 3
            NPT = len(PT)
            sbank_of = {}
            deferred = []

            def lo_of(u, c):
                m = c - 4 * u["j"]
                return 128 * m if (m > 0 and u["j"] > 0) else 0

            def emit_S(i):
                ui, c = steps[i]
                u = units[ui]
                lo = lo_of(u, c)
                bk = next_sb()
                parts = u["kparts"]
                for pi, (kfn, qap, qres) in enumerate(parts):
                    kap, kres = kfn(c)
                    mm(banks[bk][:, lo:512], kap, qap[:, lo:512], pi == 0, pi == len(parts) - 1,
                       [kres, qres], [r_bank[bk]])
                near = c >= 4 * u["j"] + u["nearmin"]
                pt = PT[i % NPT]
                if near or u["bias"] is None:
                    act(pt[:, lo:512], banks[bk][:, lo:512], ACT.Exp, [r_bank[bk]], [r_PT[i % NPT]], scale=u["scale"])
                else:
                    act(pt[:, lo:512], banks[bk][:, lo:512], ACT.Exp, [r_bank[bk], r_small], [r_PT[i % NPT]],
                        scale=u["scale"], bias=u["bias"])
                if near:
                    delta = 512 * u["j"] - 128 * c
                    st = strips[:, u["strip"], delta + 384 + lo:delta + 384 + 512]
                    tt(STRIP_ENG, pt[:, lo:512], pt[:, lo:512], st, ALU.mult, [r_PT[i % NPT], r_strips], [r_PT[i % NPT]])

            def emit_PV(i):
                ui, c = steps[i]
                u = units[ui]
                lo = lo_of(u, c)
                first = c == 0
                last = c == 4 * u["j"] + 3
                ob = 3 + 2 * (ui % 2)
                db = ob + 1
                vap, vres = u["v"](c)
                pt = PT[i % NPT]
                mm(banks[ob][:, lo:512], vap, pt[:, lo:512], first, last, [vres, r_PT[i % NPT]], [r_bank[ob]])
                if c % 4 == 3:
                    for g in range(4):
                        ii = i - 3 + g
                        cc = c - 3 + g
                        lo2 = lo_of(u, cc)
                        p2 = PT[ii % NPT]
                        S.op("pe", lambda e, g=g, lo2=lo2, p2=p2, st=(cc < 4), sp=(cc >= 4 * u["j"]), db=db: e.matmul(
                            banks[db][32 * g:32 * g + 32, lo2:512], lhsT=ones[:, 0:32], rhs=p2[:, lo2:512], start=st, stop=sp,
                            tile_position=(0, 32 * g)), [r_const, r_PT[ii % NPT]], [r_bank[db]])
                if last:
                    u["fin"](u, ob, db, lambda fn, k=3: deferred.append((i + k, fn)))

            n = len(steps)
            task_at = {}
            base = 0
            for ui, u in enumerate(units):
                ns_ = 4 * u["j"] + 4
                tk = u.get("tasks", [])
                for k, fn in enumerate(tk):
                    task_at.setdefault(base + (k * ns_) // len(tk), []).append(fn)
                base += ns_
            for i in range(n + DEPTH):
                if i < n:
                    emit_S(i)
                    for fn in task_at.get(i, []):
                        fn()
                if i >= DEPTH:
                    emit_PV(i - DEPTH)
                    due = [d for d in deferred if d[0] <= i - DEPTH]
                    for d in due:
                        deferred.remove(d)
                        d[1]()
            for d in deferred:
                d[1]()

        A.reset()
        c_qT = A.alloc(3 * S_LEN * 2, BF16, (3, S_LEN))
        c_kvT = A.alloc(2 * S_LEN * 2, BF16, (2, S_LEN))
        k_ropeT = A.alloc(S_LEN * 2, BF16)
        q_ropeT = A.alloc(2 * S_LEN * 2, BF16, (2, S_LEN))
        BOUT_END = A.off
        r_cq = [Res(f"cq{j}") for j in range(NQ)]
        r_ckv = [Res(f"ckv{j}") for j in range(NQ)]
        r_krope = [Res(f"krope{j}") for j in range(NQ)]
        r_qrope = [Res(f"qrope{j}") for j in range(NQ)]

        xs = [A.alloc(4096, F32) for _ in range(3)]
        r_xs = [Res(f"xs{i}") for i in range(3)]
        junk = A.alloc(2048, BF16)
        r_junk = Res("junk")
        xn = [A.alloc(2048, BF16) for _ in range(2)]
        r_xn = [Res(f"xn{i}") for i in range(2)]
        hTt = [A.alloc(2048, BF16, (8, 128)) for _ in range(2)]
        r_hTt = [Res(f"hTt{i}") for i in range(2)]
        Wlat = A.alloc(8 * 768 * 2, BF16, (8, 768))
        r_Wlat = [Res(f"Wlat{i}") for i in range(3)]
        Wqr = A.alloc(3 * 512 * 2, BF16, (3, 512))
        r_Wqr = [Res(f"Wqr{i}") for i in range(12)]
        latn = [A.alloc(768 * 2, BF16) for _ in range(2)]
        r_latn = [Res(f"latn{i}") for i in range(2)]
        stat = [A.alloc(64, F32) for _ in range(2)]
        r_stat = [Res(f"stat{i}") for i in range(2)]
        rtmp = [A.alloc(3 * 256, F32) for _ in range(2)]
        r_rtmp = [Res(f"rtmp{i}") for i in range(2)]
        qtmp = [A.alloc(2 * 1024, F32) for _ in range(2)]
        r_qtmp = [Res(f"qtmp{i}") for i in range(2)]
        qper = [A.alloc(512, BF16) for _ in range(2)]
        r_qper = [Res(f"qper{i}") for i in range(2)]
        assert A.off <= ROPE_OFF, A.off

        def wview(wd, c0, c1):
            return wd[:, c0:c1].rearrange("(k p) c -> p k c", p=128)

        dma("pool", Wlat[:, :, 0:704], wview(w_in_d, 1536, 2240), [], [r_Wlat[0]])
        dma("pool", Wlat[:, :, 704:736], wview(w_in_d, 2208, 2240), [], [r_Wlat[1]])
        dma("pool", Wlat[:, :, 736:768], wview(w_in_d, 2176, 2208), [], [r_Wlat[2]])
        for h in range(4):
            b = h * 192 + 128
            dma("pool", Wqr[:, :, h * 64:(h + 1) * 64], wview(w_uq_d, b, b + 64), [], [r_Wqr[3 * h]])
            dma("pool", Wqr[:, :, 256 + h * 64:256 + h * 64 + 32], wview(w_uq_d, b + 32, b + 64), [], [r_Wqr[3 * h + 1]])
            dma("pool", Wqr[:, :, 256 + h * 64 + 32:256 + h * 64 + 64], wview(w_uq_d, b, b + 32), [], [r_Wqr[3 * h + 2]])

        def norm_stages(gain_ap, dst_fn, xs, r_xs, xn, r_xn, junk, r_junk, stat, r_stat, tp_banks, reuse_rstd=False):
            def st_load(t):
                b = t % len(xs)
                dma("sp", xs[b], x_d[t * 128:(t + 1) * 128, :], [], [r_xs[b]])

            def st_stat(t):
                b = t % len(xs)
                sa = stat[t % 2]
                act(junk, xs[b], ACT.Square, [r_xs[b]], [r_junk, r_stat[t % 2]], accum_out=sa[:, 0:1])
                act(sa[:, 1:2], sa[:, 0:1], ACT.Ln, [r_stat[t % 2]], [r_stat[t % 2]], scale=1.0 / 1024.0, bias=EPS)
                act(rstd_all[:, t:t + 1], sa[:, 1:2], ACT.Exp, [r_stat[t % 2]], [r_rstd[t]], scale=-0.5)

            def st_xn(t):
                b = t % len(xs)
                ts("dve", xn[t % 2], xs[b], rstd_all[:, t:t + 1], None, ALU.mult, None, [r_xs[b], r_rstd[t]], [r_xn[t % 2]])

            def st_tr(t):
                bk = tp_banks[t % len(tp_banks)]
                tpv = bank_bf(bk).rearrange("p (a b) -> p a b", a=8, b=128)
                for c in range(8):
                    tr(tpv[:, c, :], xn[t % 2][:, c * 128:(c + 1) * 128], [r_xn[t % 2]], [r_bank[bk]])

            def st_evac(t):
                bk = tp_banks[t % len(tp_banks)]
                tpv = bank_bf(bk).rearrange("p (a b) -> p a b", a=8, b=128)
                dst, rdst = dst_fn(t)
                tt("dve", dst, tpv, gain_ap.unsqueeze(2).to_broadcast([128, 8, 128]), ALU.mult,
                   [r_bank[bk], r_const], [rdst])
            if reuse_rstd:
                return [st_load, st_xn, st_tr, st_evac]
            return [st_load, st_stat, st_xn, st_tr, st_evac]

        nst1 = norm_stages(g_attn, lambda t: (hTt[t % 2], r_hTt[t % 2]), xs, r_xs, xn, r_xn, junk, r_junk,
                           stat, r_stat, [0, 1])

        statB = [A.alloc(64, F32) for _ in range(2)]
        r_statB = [Res(f"statB{i}") for i in range(2)]
        assert A.off <= ROPE_OFF, A.off

        def st_latmm(t):
            h = hTt[t % 2]
            b0 = 2 + 2 * (t % 2)
            b1 = b0 + 1
            for k in range(8):
                mm(banks[b0][:, 0:384], h[:, k, :], Wlat[:, k, 0:384], k == 0, k == 7, [r_hTt[t % 2]] + r_Wlat, [r_bank[b0]])
            for k in range(8):
                mm(banks[b1][:, 0:384], h[:, k, :], Wlat[:, k, 384:768], k == 0, k == 7, [r_hTt[t % 2]] + r_Wlat, [r_bank[b1]])

        def st_latel(t):
            b0 = 2 + 2 * (t % 2)
            b1 = b0 + 1
            sa = statB[t % 2]
            ln = latn[t % 2]
            rs = [r_statB[t % 2]]
            act(junk[:, 0:384], banks[b0][:, 0:384], ACT.Square, [r_bank[b0]], [r_junk] + rs, accum_out=sa[:, 4:5])
            act(junk[:, 0:256], banks[b1][:, 0:256], ACT.Square, [r_bank[b1]], [r_junk] + rs, accum_out=sa[:, 5:6])
            rt = rtmp[t % 2]
            tt("dve", rt[:, 0:64], banks[b1][:, 256:320], C2[:, t, :], ALU.mult, [r_bank[b1], r_C2], [r_rtmp[t % 2]])
            tt("dve", rt[:, 64:128], banks[b1][:, 320:384], S2[:, t, :], ALU.mult, [r_bank[b1], r_S2], [r_rtmp[t % 2]])
            tt("dve", ln[:, 640:704], rt[:, 0:64], rt[:, 64:128], ALU.add, [r_rtmp[t % 2]], [r_latn[t % 2]])
            tt("dve", ln[:, 704:768], rt[:, 0:64], rt[:, 64:128], ALU.add, [r_rtmp[t % 2]], [r_latn[t % 2]])
            rsqrt_act(sa[:, 8:9], sa[:, 4:5], 384.0, rs, rs, sa[:, 6:7], r_statB[t % 2])
            rsqrt_act(sa[:, 9:10], sa[:, 5:6], 256.0, rs, rs, sa[:, 7:8], r_statB[t % 2])
            ts("dve", ln[:, 0:384], banks[b0][:, 0:384], sa[:, 8:9], None, ALU.mult, None, [r_bank[b0]] + rs, [r_latn[t % 2]])
            act(ln[:, 384:640], banks[b1][:, 0:256], ACT.Copy, [r_bank[b1]] + rs, [r_latn[t % 2]], scale=sa[:, 9:10])

        def tp2v():
            return bank_bf(6).rearrange("p (a b) -> p a b", a=8, b=128)

        def st_tr2(t):
            ln = latn[t % 2]
            tpv = tp2v()
            for c in range(6):
                tr(tpv[:, c, :], ln[:, c * 128:(c + 1) * 128], [r_latn[t % 2]], [r_bank[6]])

        def st_ev2(t):
            tpv = tp2v()
            j = t // 4
            cols = slice(t * 128, (t + 1) * 128)
            tt("dve", c_qT[:, :, cols], tpv[:, 0:3, :], g_q.unsqueeze(2).to_broadcast([128, 3, 128]), ALU.mult,
               [r_bank[6], r_const], [r_cq[j]])
            tt("dve", c_kvT[:, :, cols], tpv[:, 3:5, :], g_kv.unsqueeze(2).to_broadcast([128, 2, 128]), ALU.mult,
               [r_bank[6], r_const], [r_ckv[j]])
            cp("act", k_ropeT[:, cols], tpv[:, 5, :], [r_bank[6]], [r_krope[j]])

        def st_qpe(t):
            j = t // 4
            cols = slice(t * 128, (t + 1) * 128)
            for k in range(3):
                mm(banks[7][:, :], c_qT[:, k, cols], Wqr[:, k, :], k == 0, k == 2, [r_cq[j]] + r_Wqr, [r_bank[7]])

        def st_qrope(t):
            qt = qtmp[t % 2]
            a3 = qt[:, 0:256].rearrange("p (a b) -> p a b", a=4, b=64)
            b3 = qt[:, 256:512].rearrange("p (a b) -> p a b", a=4, b=64)
            p3 = banks[7][:, 0:256].rearrange("p (a b) -> p a b", a=4, b=64)
            s3 = banks[7][:, 256:512].rearrange("p (a b) -> p a b", a=4, b=64)
            tt("dve", a3, p3, C2[:, t, :].unsqueeze(1).to_broadcast([128, 4, 64]), ALU.mult, [r_bank[7], r_C2], [r_qtmp[t % 2]])
            tt("dve", b3, s3, S2[:, t, :].unsqueeze(1).to_broadcast([128, 4, 64]), ALU.mult, [r_bank[7], r_S2], [r_qtmp[t % 2]])
            tt("dve", qper[t % 2], qt[:, 0:256], qt[:, 256:512], ALU.add, [r_qtmp[t % 2]], [r_qper[t % 2]])

        def st_tr3(t):
            tpv = tp2v()
            for c in range(2):
                tr(tpv[:, 6 + c, :], qper[t % 2][:, c * 128:(c + 1) * 128], [r_qper[t % 2]], [r_bank[6]])

        def st_cp3(t):
            j = t // 4
            cols = slice(t * 128, (t + 1) * 128)
            tpv = tp2v()
            cp("act", q_ropeT[:, :, cols], tpv[:, 6:8, :], [r_bank[6]], [r_qrope[j]])

        skew(NT, nst1 + [st_latmm, st_latel, st_tr2, st_ev2, st_qpe, st_qrope, st_tr3, st_cp3])
        dbg_dump("cqT", c_qT.rearrange("p a b -> p (a b)"), 3 * S_LEN, BF16)
        dbg_dump("ckvT", c_kvT.rearrange("p a b -> p (a b)"), 2 * S_LEN, BF16)
        dbg_dump("kropeT", k_ropeT, S_LEN, BF16)
        dbg_dump("qropeT", q_ropeT.rearrange("p a b -> p (a b)"), 2 * S_LEN, BF16)
        dbg_dump("C2", C2.rearrange("p a b -> p (a b)"), 2048, F32)
        dbg_dump("S2", S2.rearrange("p a b -> p (a b)"), 2048, F32)

        S.barrier()
        A.reset(BOUT_END)
        q_nT = A.alloc(S_LEN * 2, BF16)
        k_nT = A.alloc(S_LEN * 2, BF16)
        Vh = A.alloc(S_LEN * 2, BF16, (32, 128))
        r_qn = [Res(f"qn{j}") for j in range(NQ)]
        r_kn = [Res(f"kn{j}") for j in range(NQ)]
        r_V = [Res(f"V{j}") for j in range(NQ)]
        Wuqn = A.alloc(3 * 512 * 2, BF16, (3, 512))
        Wukn = A.alloc(2 * 512 * 2, BF16, (2, 512))
        Wukv = A.alloc(2 * 512 * 2, BF16, (2, 512))
        r_Wm = [Res(f"Wmla{i}") for i in range(12)]
        PT = [A.alloc(1024, BF16) for _ in range(8)]
        r_PT = [Res(f"PT{i}") for i in range(8)]
        Rt = [A.alloc(2048, F32) for _ in range(2)]
        r_Rt = [Res(f"Rt{i}") for i in range(2)]
        Dsb = [A.alloc(2048, F32) for _ in range(2)]
        r_Dsb = [Res(f"Dsb{i}") for i in range(2)]
        kr_hi = A.alloc(S_LEN * 2, BF16)
        kr_lo = k_ropeT
        S.op("dve", lambda e: e.memset(kr_hi[0:64, :], 0.0), [], r_krope)
        S.op("dve", lambda e: e.tensor_copy(out=kr_hi[64:128, :], in_=k_ropeT[64:128, :]), r_krope, r_krope)
        S.op("dve", lambda e: e.memset(k_ropeT[64:128, :], 0.0), r_krope, r_krope)
        kr_pad = [kr_lo, kr_hi]
        for h in range(4):
            dma("pool", Wuqn[:, :, h * 128:(h + 1) * 128], wview(w_uq_d, h * 192, h * 192 + 128), [], [r_Wm[3 * h]])
            dma("pool", Wukn[:, :, h * 128:(h + 1) * 128], wview(w_ukv_d, h * 256, h * 256 + 128), [], [r_Wm[3 * h + 1]])
            dma("pool", Wukv[:, :, h * 128:(h + 1) * 128], wview(w_ukv_d, h * 256 + 128, h * 256 + 256), [], [r_Wm[3 * h + 2]])

        r_w1b = [Res(f"w1b{i}") for i in range(8)]
        r_w2b = [Res(f"w2b{i}") for i in range(8)]
        r_wob = [Res(f"wob{i}") for i in range(2)]
        for i in range(2):
            dma("pool", wob_d[i * 512:(i + 1) * 512, :], w_out_d[i * 512:(i + 1) * 512, :], [], [r_wob[i]])
        for i in range(8):
            dma("pool", w1b_d[i * 128:(i + 1) * 128, :], w1_d[i * 128:(i + 1) * 128, :], [], [r_w1b[i]])
        for i in range(8):
            dma("pool", w2b_d[i * 512:(i + 1) * 512, :], w2_d[i * 512:(i + 1) * 512, :], [], [r_w2b[i]])

        proj_ring = [0, 1, 2, 7]
        pr_ctr = [0]

        def next_pb():
            return next_sb()

        fin_ctr = [0]
        MLA_SCALE = 192.0 ** -0.5
        for h in range(4):
            def mla_tasks(j, h=h):
                cols = slice(j * 512, (j + 1) * 512)

                def t_q():
                    b = next_pb()
                    for k in range(3):
                        mm(banks[b][:, :], Wuqn[:, k, h * 128:(h + 1) * 128], c_qT[:, k, cols], k == 0, k == 2,
                           r_Wm + [r_cq[j]], [r_bank[b]])
                    cp("dve", q_nT[:, cols], banks[b][:, :], [r_bank[b]], [r_qn[j]])

                def t_k():
                    b = next_pb()
                    for k in range(2):
                        mm(banks[b][:, :], Wukn[:, k, h * 128:(h + 1) * 128], c_kvT[:, k, cols], k == 0, k == 1,
                           r_Wm + [r_ckv[j]], [r_bank[b]])
                    cp("dve", k_nT[:, cols], banks[b][:, :], [r_bank[b]], [r_kn[j]])

                def t_v():
                    b = next_pb()
                    bv = banks[b][:, :].rearrange("p (a b) -> p a b", a=4, b=128)
                    for tt_ in range(4):
                        t = 4 * j + tt_
                        for k in range(2):
                            mm(bv[:, tt_, :], c_kvT[:, k, t * 128:(t + 1) * 128], Wukv[:, k, h * 128:(h + 1) * 128],
                               k == 0, k == 1, r_Wm + [r_ckv[j]], [r_bank[b]])
                    cp("dve", Vh[:, 4 * j:4 * j + 4, :], bv, [r_bank[b]], [r_V[j]])
                return [t_q, t_k, t_v]

            for fn in mla_tasks(0):
                fn()

            half = (h % 2) * 64
            pair = h // 2

            def fin_mla(u, ob, db, later, h=h):
                j = u["j"]
                cols = slice(j * 512, (j + 1) * 512)
                f = fin_ctr[0] % 2
                fin_ctr[0] += 1
                cp("dve", Dsb[f], banks[db][:, :], [r_bank[db]], [r_Dsb[f]])

                def tail(f=f, j=j, cols=cols, ob=ob):
                    mb = next_sb()
                    mm(banks[mb][:, :], c32[:, :], Dsb[f], True, True, [r_const, r_Dsb[f]], [r_bank[mb]])
                    act(Rt[f], banks[mb][:, :], ACT.Ln, [r_bank[mb]], [r_Rt[f]])
                    act(Rt[f], Rt[f], ACT.Exp, [r_Rt[f]], [r_Rt[f]], scale=-1.0)
                    tt("dve", mixT[:, 4 + h, cols], banks[ob][:, :], Rt[f], ALU.mult, [r_bank[ob], r_Rt[f]], [r_mix[4 + h][j]])
                later(tail, 2)

            units = []
            for j in range(NQ):
                cols = slice(j * 512, (j + 1) * 512)
                units.append(dict(
                    j=j, scale=MLA_SCALE, bias=None, strip=4, nearmin=0,
                    kparts=[
                        (lambda c: (k_nT[:, c * 128:(c + 1) * 128], r_kn[c // 4]), q_nT[:, cols], r_qn[j]),
                        (lambda c, hh=h % 2: (kr_pad[hh][:, c * 128:(c + 1) * 128], r_krope[c // 4]),
                         q_ropeT[:, pair, cols], r_qrope[j]),
                    ],
                    v=lambda c: (Vh[:, c, :], r_V[c // 4]),
                    tasks=(mla_tasks(j + 1) if j + 1 < NQ else []),
                    fin=fin_mla))
            run_attention(units, PT, r_PT)
            if h == 0:
                dbg_dump("qnT", q_nT, S_LEN, BF16)
                dbg_dump("knT", k_nT, S_LEN, BF16)
                dbg_dump("Vh", Vh.rearrange("p a b -> p (a b)"), S_LEN, BF16)

        S.barrier()
        A.reset()
        hT = A.alloc(8 * S_LEN * 2, BF16, (8, S_LEN))
        r_hT = [Res(f"hT{j}") for j in range(NQ)]
        HT_END = A.off
        xs = [A.alloc(4096, F32) for _ in range(6)]
        r_xs = [Res(f"xsb{i}") for i in range(6)]
        junk = A.alloc(2048, BF16)
        r_junk = Res("junkb")
        xn = [A.alloc(2048, BF16) for _ in range(2)]
        r_xn = [Res(f"xnb{i}") for i in range(2)]
        stat = [A.alloc(64, F32) for _ in range(2)]
        r_stat = [Res(f"statb{i}") for i in range(2)]
        assert A.off <= HT_END + 4 * S_LEN * 2, A.off
        A.reset(HT_END + 4 * S_LEN * 2)
        Wh = A.alloc(8 * 384 * 2, BF16, (8, 384))
        r_Wh = [Res(f"Wh{i}") for i in range(3)]

        def load_Wh(h):
            for part in range(3):
                c0 = part * 512 + h * 128
                dma("pool", Wh[:, :, part * 128:(part + 1) * 128], wview(w_in_d, c0, c0 + 128), [], [r_Wh[part]])

        load_Wh(0)
        nst2 = norm_stages(g_attn, lambda t: (hT[:, :, t * 128:(t + 1) * 128], r_hT[t // 4]), xs, r_xs, xn, r_xn,
                           junk, r_junk, stat, r_stat, [0, 1], reuse_rstd=True)
        skew(NT, nst2)

        S.barrier()
        A.reset(HT_END)
        qT = A.alloc(S_LEN * 2, BF16)
        kT0 = A.alloc(S_LEN * 2, BF16)
        kT1 = A.alloc(S_LEN * 2, BF16)
        kTs = [kT0, kT1]
        Vd = A.alloc(S_LEN * 2, BF16, (32, 128))
        r_q = [Res(f"q{j}") for j in range(NQ)]
        r_k = [Res(f"k{j}") for j in range(NQ)]
        r_Vd = [Res(f"Vd{j}") for j in range(NQ)]
        _wh2 = A.alloc(8 * 384 * 2, BF16, (8, 384))
        PT = [A.alloc(1024, BF16) for _ in range(8)]
        r_PT = [Res(f"PTd{i}") for i in range(8)]
        On = [A.alloc(2048, F32) for _ in range(3)]
        r_On = [Res(f"On{i}") for i in range(3)]
        Dsb = A.alloc(2048, F32)
        r_Dsbd = Res("Dsbd")
        Ct = [A.alloc(2048, F32) for _ in range(2)]
        r_Ct = [Res(f"Ct{i}") for i in range(2)]
        sqt = [A.alloc(1024, BF16) for _ in range(2)]
        r_sqt = [Res(f"sqt{i}") for i in range(2)]
        lnt = [A.alloc(2048, F32) for _ in range(2)]
        r_lnt = [Res(f"lnt{i}") for i in range(2)]
        S.op("dve", lambda e: e.memset(kT0[64:128, :], 0.0), [], r_k)
        S.op("dve", lambda e: e.memset(kT1[0:64, :], 0.0), [], r_k)

        pair_ctr = [0]
        DIFF_SCALE = 64.0 ** -0.5
        for h in range(4):
            def diff_tasks(j, h=h):
                w = Wh
                rw = r_Wh
                cols = slice(j * 512, (j + 1) * 512)

                def t_q():
                    b = next_pb()
                    for k in range(8):
                        mm(banks[b][:, :], w[:, k, 0:128], hT[:, k, cols], k == 0, k == 7, rw + [r_hT[j]], [r_bank[b]])
                    cp("dve", qT[:, cols], banks[b][:, :], [r_bank[b]], [r_q[j]])

                def t_k():
                    b = next_pb()
                    for k in range(8):
                        mm(banks[b][:, :], w[:, k, 128:256], hT[:, k, cols], k == 0, k == 7, rw + [r_hT[j]], [r_bank[b]])
                    cp("dve", kT0[0:64, cols], banks[b][0:64, :], [r_bank[b]], [r_k[j]])
                    cp("dve", kT1[64:128, cols], banks[b][64:128, :], [r_bank[b]], [r_k[j]])

                def t_v(lo_, hi_):
                    def f():
                        b = next_pb()
                        bv = banks[b][:, :].rearrange("p (a b) -> p a b", a=4, b=128)
                        for tt_ in range(lo_, hi_):
                            t = 4 * j + tt_
                            for k in range(8):
                                mm(bv[:, tt_, :], hT[:, k, t * 128:(t + 1) * 128], w[:, k, 256:384], k == 0, k == 7,
                                   rw + [r_hT[j]], [r_bank[b]])
                        cp("dve", Vd[:, 4 * j + lo_:4 * j + hi_, :], bv[:, lo_:hi_, :], [r_bank[b]], [r_Vd[j]])
                    return f
                return [t_q, t_k, t_v(0, 2), t_v(2, 4)]

            for fn in diff_tasks(0):
                fn()

            def fin_diff(u, ob, db, later, h=h):
                j = u["j"]
                comp = u["comp"]
                cols = slice(j * 512, (j + 1) * 512)
                pi = pair_ctr[0] % 2
                oi = pi if comp == 0 else 2
                if comp == 1:
                    pair_ctr[0] += 1
                cp("dve", Dsb, banks[db][:, :], [r_bank[db]], [r_Dsbd])

                def tail(pi=pi, oi=oi, comp=comp, j=j, cols=cols, ob=ob):
                    mb = next_sb()
                    mm(banks[mb][:, :], c32[:, :], Dsb, True, True, [r_const, r_Dsbd], [r_bank[mb]])
                    act(On[oi], banks[mb][:, :], ACT.Ln, [r_bank[mb]], [r_On[oi]])
                    act(On[oi], On[oi], ACT.Exp, [r_On[oi]], [r_On[oi]], scale=-1.0)
                    tt("dve", On[oi], banks[ob][:, :], On[oi], ALU.mult, [r_bank[ob], r_On[oi]], [r_On[oi]])
                    if comp == 1:
                        C = Ct[pi]
                        stt("dve", C, On[2], small[:, 4:5], On[pi], ALU.mult, ALU.add,
                            [r_On[2], r_On[pi], r_small], [r_Ct[pi]])
                        tt("dve", sqt[pi], C, C, ALU.mult, [r_Ct[pi]], [r_sqt[pi]])

                        def tail2(pi=pi, j=j, cols=cols):
                            mb = next_sb()
                            mm(banks[mb][:, :], ones[:, :], sqt[pi], True, True, [r_const, r_sqt[pi]], [r_bank[mb]])
                            act(lnt[pi], banks[mb][:, :], ACT.Ln, [r_bank[mb]], [r_lnt[pi]], scale=1.0 / 128.0, bias=EPS)
                            act(lnt[pi], lnt[pi], ACT.Exp, [r_lnt[pi]], [r_lnt[pi]], scale=-0.5)
                            stt("dve", mixT[:, h, cols], Ct[pi], small[:, 5:6], lnt[pi], ALU.mult, ALU.mult,
                                [r_Ct[pi], r_small, r_lnt[pi]], [r_mix[h][j]])
                        later(tail2, 6)
                later(tail, 2)

            units = []
            for j in range(NQ):
                cols = slice(j * 512, (j + 1) * 512)
                for comp in range(2):
                    units.append(dict(
                        j=j, comp=comp, scale=DIFF_SCALE, bias=small[:, h:h + 1], strip=h, nearmin=-1,
                        kparts=[(lambda c, comp=comp: (kTs[comp][:, c * 128:(c + 1) * 128], r_k[c // 4]),
                                 qT[:, cols], r_q[j])],
                        v=lambda c: (Vd[:, c, :], r_Vd[c // 4]),
                        tasks=(((diff_tasks(j + 1)[2 * comp:2 * comp + 2]) if j + 1 < NQ else [])
                               + ([lambda h=h: load_Wh(h + 1)] if (j + 2 == NQ and comp == 1 and h + 1 < 4) else [])),
                        fin=fin_diff))
            run_attention(units, PT, r_PT)

        if dump_mix:
            S.barrier()
            dma("sp", mixdump_d[:, :], mixT[:, :, :].rearrange("p a b -> p (a b)"), [r for rr in r_mix for r in rr], [],
                tag="out")

        A.reset()
        Wo = A.alloc(8 * 1024 * 2, BF16, (8, 1024))
        r_Wo = Res("Wo")
        xb = [A.alloc(4096, F32) for _ in range(8)]
        r_xb = [Res(f"xb{i}") for i in range(8)]
        xn = [A.alloc(2048, BF16) for _ in range(2)]
        r_xn = [Res(f"xnc{i}") for i in range(2)]
        junk = A.alloc(2048, BF16)
        r_junk = Res("junkc")
        h2T = A.alloc(8 * 512 * 2, BF16, (8, 512))
        r_h2T = [Res(f"h2T{s}") for s in range(4)]
        uT = A.alloc(32 * 512 * 2, BF16, (32, 512))
        r_uT = [Res(f"uT{c}") for c in range(32)]
        rl = [A.alloc(2048, F32) for _ in range(2)]
        r_rl = [Res("rl0"), Res("rl1")]
        W1s = [A.alloc(8 * 512 * 2, BF16, (8, 512)) for _ in range(2)]
        r_W1s = [Res("W1s0"), Res("W1s1")]
        W2s = [A.alloc(4 * 512 * 2, BF16, (4, 512)) for _ in range(2)]
        r_W2s = [Res("W2s0"), Res("W2s1")]
        gfin = A.alloc(4096, F32)
        r_gfin = Res("gfin")
        stat = [A.alloc(64, F32) for _ in range(2)]
        r_stat = [Res(f"statc{i}") for i in range(2)]

        dma("sp", Wo, wview(wob_d, 0, 1024), r_wob, [r_Wo] + r_hT)
        for s_ in range(4):
            dma("pool", xb[s_], x_d[s_ * 128:(s_ + 1) * 128, :], [], [r_xb[s_]] + r_hT)
        S.barrier()
        dma("sp", gfin, bass.AP(gfin_d, 0, [[0, 128], [1, 1024]]), [], [r_gfin])

        w1_ctr = [0]
        w2_ctr = [0]
        ffn_ring = [0]
        st_ctr = [0]

        def op_load(i):
            for s_ in range(4):
                t = 4 * i + s_
                bi = (4 * i + s_) % 8
                dma("pool", xb[bi], x_d[t * 128:(t + 1) * 128, :], [], [r_xb[bi]])

        def op_a(i, s_):
            t = 4 * i + s_
            bi = (4 * i + s_) % 8
            buf = xb[bi]
            for half in range(2):
                b = 4 + half
                for k in range(8):
                    mm(banks[b][:, :], mixT[:, k, t * 128:(t + 1) * 128], Wo[:, k, half * 512:(half + 1) * 512],
                       k == 0, k == 7, [r_mix[k][i], r_Wo], [r_bank[b]])
                hs = slice(half * 512, (half + 1) * 512)
                tt("dve", buf[:, hs], buf[:, hs], banks[b][:, :], ALU.add, [r_xb[bi], r_bank[b]], [r_xb[bi]])

        def op_b(i, s_):
            bi = (4 * i + s_) % 8
            buf = xb[bi]
            si = st_ctr[0] % 2
            st_ctr[0] += 1
            sa = stat[si]
            rs = [r_stat[si]]
            act(junk, buf, ACT.Square, [r_xb[bi]], [r_junk] + rs, accum_out=sa[:, 0:1])
            rsqrt_act(sa[:, 2:3], sa[:, 0:1], 1024.0, rs, rs, sa[:, 1:2], r_stat[si])
            ts("dve", xn[s_ % 2], buf, sa[:, 2:3], None, ALU.mult, None, [r_xb[bi]] + rs, [r_xn[s_ % 2]])

        def op_c(i, s_):
            tpv = bank_bf(6).rearrange("p (a b) -> p a b", a=8, b=128)
            for c in range(8):
                tr(tpv[:, c, :], xn[s_ % 2][:, c * 128:(c + 1) * 128], [r_xn[s_ % 2]], [r_bank[6]])
            tt("dve", h2T[:, :, s_ * 128:(s_ + 1) * 128], tpv, g_mlp.unsqueeze(2).to_broadcast([128, 8, 128]), ALU.mult,
               [r_bank[6], r_const], [r_h2T[s_]])

        OP_SLOTS = {0: [("a", 0)], 1: [("b", 0), ("a", 1)], 2: [("c", 0), ("b", 1), ("a", 2)],
                    3: [("c", 1), ("b", 2), ("a", 3)], 4: [("c", 2), ("b", 3)], 5: [("c", 3)]}

        def op_slot(i, slot):
            for kind, s_ in OP_SLOTS.get(slot, []):
                {"a": op_a, "b": op_b, "c": op_c}[kind](i, s_)

        def FFN1(i):
            if i + 1 < NQ:
                op_load(i + 1)
            for g in range(8):
                wi = w1_ctr[0] % 2
                w1_ctr[0] += 1
                dma("sp", W1s[wi], wview(w1b_d, g * 512, (g + 1) * 512), r_w1b, [r_W1s[wi]])
                for cc in range(4):
                    c = 4 * g + cc
                    b = ffn_ring[0] % 8
                    ffn_ring[0] += 1
                    for k in range(8):
                        mm(banks[b][:, :], W1s[wi][:, k, cc * 128:(cc + 1) * 128], h2T[:, k, :], k == 0, k == 7,
                           [r_W1s[wi]] + r_h2T, [r_bank[b]])
                    ri = c % 2
                    act(rl[ri], banks[b][:, :], ACT.Relu, [r_bank[b]], [r_rl[ri]])
                    tt("dve", uT[:, c, :], rl[ri], rl[ri], ALU.mult, [r_rl[ri]], [r_uT[c]])

        def FFN2(i):
            for half in range(2):
                hs = slice(half * 512, (half + 1) * 512)
                for g in range(8):
                    wi = w2_ctr[0] % 2
                    w2_ctr[0] += 1
                    dma("sp", W2s[wi], w2b_d[g * 512:(g + 1) * 512, hs].rearrange("(k p) c -> p k c", p=128), [r_w2b[g]],
                        [r_W2s[wi]])
                    for s in range(4):
                        b = 4 * half + s
                        for cc in range(4):
                            c = 4 * g + cc
                            mm(banks[b][:, :], uT[:, c, s * 128:(s + 1) * 128], W2s[wi][:, cc, :], c == 0, c == 31,
                               [r_uT[c], r_W2s[wi]], [r_bank[b]])
                    if half == 0 and i + 1 < NQ:
                        op_slot(i + 1, g)
                for s in range(4):
                    b = 4 * half + s
                    bi = (4 * i + s) % 8
                    tt("dve", xb[bi][:, hs], xb[bi][:, hs], banks[b][:, :], ALU.add, [r_xb[bi], r_bank[b]], [r_xb[bi]])
            for s in range(4):
                t = 4 * i + s
                bi = (4 * i + s) % 8
                buf = xb[bi]
                si = st_ctr[0] % 2
                st_ctr[0] += 1
                sa = stat[si]
                rs = [r_stat[si]]
                act(junk, buf, ACT.Square, [r_xb[bi]], [r_junk] + rs, accum_out=sa[:, 0:1])
                rsqrt_act(sa[:, 2:3], sa[:, 0:1], 1024.0, rs, rs, sa[:, 1:2], r_stat[si])
                stt("dve", buf, buf, sa[:, 2:3], gfin, ALU.mult, ALU.mult, [r_xb[bi], r_gfin] + rs, [r_xb[bi]])
                dma("pool", y_d[t * 128:(t + 1) * 128, :], buf, [r_xb[bi]], [], tag="out")

        for slot in range(6):
            op_slot(0, slot)
        for i in range(NQ):
            FFN1(i)
            FFN2(i)

        S.finalize()
        csems = {e: es.enter_context(nc.semaphore("c_" + e)) for e in ("pe", "act", "dve", "pool")}
        dsems = {}
        for q, n in S.nslots.items():
            for i in range(n):
                dsems[(q, i)] = es.enter_context(nc.semaphore(f"d_{q}_{i}"))
        block = es.enter_context(nc.Block())

        @block.tensor
        def _(e):
            S.emit_engine("pe", e, csems, dsems)

        @block.scalar
        def _(e):
            S.emit_engine("act", e, csems, dsems)

        @block.vector
        def _(e):
            S.emit_engine("dve", e, csems, dsems)

        @block.gpsimd
        def _(e):
            S.emit_engine("pool", e, csems, dsems)

        @block.sync
        def _(e):
            S.emit_engine("sp", e, csems, dsems)

    return nc, S


def _col_layout(v, nch):
    return np.ascontiguousarray(np.asarray(v, np.float32).reshape(nch, 128).T)


def make_in_maps(inputs):
    f = lambda a: np.ascontiguousarray(np.asarray(a, np.float32))
    x = f(inputs["x"])
    gains = np.concatenate([
        _col_layout(inputs["norm_attn"][0], 8), _col_layout(inputs["norm_mlp"][0], 8),
        _col_layout(inputs["mla_q_norm"][0], 3), _col_layout(inputs["mla_kv_norm"][0], 2),
        _col_layout(inputs["diff_subln"][0], 1)], axis=1)
    lamv = np.concatenate([f(inputs["diff_lq1"][0]), f(inputs["diff_lq2"][0]),
                           f(inputs["diff_lk1"][0]), f(inputs["diff_lk2"][0])])[None, :]
    pos = np.ascontiguousarray(np.asarray(inputs["positions"], np.int32).reshape(NT, 128).T)
    invf = (np.float32(10000.0) ** (-np.arange(0, 64, 2, dtype=np.float32) / np.float32(64)))[None, :].astype(np.float32)
    shared = {
        "w_in": f(inputs["w_in"][0]), "w_uq": f(inputs["mla_w_uq"][0]), "w_ukv": f(inputs["mla_w_ukv"][0]),
        "w_out": f(inputs["w_out"][0]), "w1": f(inputs["w_mlp_in"][0]), "w2": f(inputs["w_mlp_out"][0]),
        "gains": np.ascontiguousarray(gains), "gfin": f(inputs["norm_final"])[None, :],
        "relb": f(inputs["rel_bias"]), "lamv": np.ascontiguousarray(lamv), "pos": pos, "invf": invf,
        "oh": _onehot_table(), "ident": np.eye(128, dtype=np.float32).astype(ml_dtypes.bfloat16),
        "antiid": np.ascontiguousarray(np.eye(128, dtype=np.float32)[::-1]).astype(ml_dtypes.bfloat16),
    }
    return [dict(shared, x=np.ascontiguousarray(x[b])) for b in range(x.shape[0])]


def kernel(**inputs):
    in_maps = make_in_maps(inputs)
    nc, _ = build_program()
    res = run_bass_kernel_spmd(nc, in_maps, core_ids=list(range(8)))
    return np.stack([np.asarray(r["y"], np.float32) for r in res.results], axis=0)
```

```python
import math
import os
from contextlib import ExitStack

import numpy as np
import ml_dtypes

import concourse.bass as bass
import concourse.mybir as mybir
from concourse.bass_utils import run_bass_kernel_spmd

F32 = mybir.dt.float32
BF16 = mybir.dt.bfloat16
I32 = mybir.dt.int32
ACT = mybir.ActivationFunctionType
ALU = mybir.AluOpType
AX = mybir.AxisListType

S_LEN = 4096
D_MODEL = 1024
NT = S_LEN // 128
NQ = S_LEN // 512
EPS = 1e-6
TAB_L = 1151


class Res:
    __slots__ = ("name", "psum", "w", "r", "rd")

    def __init__(self, name, psum=False):
        self.name = name
        self.psum = psum
        self.w = None
        self.r = {}
        self.rd = []


class Op:
    __slots__ = ("eng", "fn", "idx", "deps_c", "deps_d", "signal", "is_dma",
                 "dsem", "dval", "prevval", "sval", "tag")

    def __init__(self, eng, fn, is_dma, tag=None):
        self.eng = eng
        self.fn = fn
        self.is_dma = is_dma
        self.deps_c = {}
        self.deps_d = []
        self.signal = False
        self.dsem = None
        self.dval = 0
        self.prevval = 0
        self.sval = 0
        self.idx = -1
        self.tag = tag


class Sched:
    ENGS = ("pe", "act", "dve", "pool", "sp")

    def __init__(self, ndma_slots=None):
        self.streams = {e: [] for e in self.ENGS}
        self.nslots = ndma_slots or {"sp": 12, "pool": 40}
        self.last_c = {}
        self.dmas_since = []
        self.pending = {}

    def _dep(self, op, d, kind):
        if d is None or d is op:
            return
        if (not d.is_dma) and (not op.is_dma) and d.eng == op.eng:
            if op.eng == "pe" or kind == "bar":
                return
        if d.is_dma:
            if d not in op.deps_d:
                op.deps_d.append(d)
        else:
            cur = op.deps_c.get(d.eng)
            if cur is None or d.idx > cur.idx:
                op.deps_c[d.eng] = d

    def _record(self, op, reads, writes):
        writes = list(writes)
        rds = []
        for R in reads:
            if R.psum:
                if R not in writes:
                    writes.append(R)
            else:
                rds.append(R)
        pend = self.pending.pop(op.eng, None)
        if pend is not None:
            for d in pend:
                self._dep(op, d, "bar")
        for R in rds:
            self._dep(op, R.w, "raw")
        for R in writes:
            self._dep(op, R.w, "waw")
            for rd in R.r.values():
                self._dep(op, rd, "war")
            for rd in R.rd:
                self._dep(op, rd, "war")
        for R in rds:
            if R in writes:
                continue
            if op.is_dma:
                R.rd.append(op)
            else:
                R.r[op.eng] = op
        for R in writes:
            R.w = op
            R.r = {}
            R.rd = []
        st = self.streams[op.eng]
        op.idx = len(st)
        st.append(op)
        if op.is_dma:
            self.dmas_since.append(op)
        else:
            self.last_c[op.eng] = op
        return op

    def op(self, eng, fn, reads=(), writes=(), tag=None):
        return self._record(Op(eng, fn, False, tag), reads, writes)

    def dma(self, queue, fn, reads=(), writes=(), tag=None):
        return self._record(Op(queue, fn, True, tag), reads, writes)

    def barrier(self):
        deps = list(self.last_c.values()) + list(self.dmas_since)
        for e in self.ENGS:
            old = self.pending.get(e)
            self.pending[e] = (old or []) + deps if old else list(deps)
        self.dmas_since = []

    def finalize(self):
        for e in self.ENGS:
            for op in self.streams[e]:
                for d in op.deps_c.values():
                    d.signal = True
        self.stats = {}
        for e in self.ENGS:
            cnt = 0
            i = 0
            n = self.nslots.get(e, 1)
            for op in self.streams[e]:
                if op.is_dma:
                    op.dsem = (e, i % n)
                    op.dval = 16 * (i // n + 1)
                    op.prevval = 16 * (i // n)
                    i += 1
                elif op.signal:
                    cnt += 1
                    op.sval = cnt
            self.stats[e] = (len(self.streams[e]), cnt, i)

    def emit_engine(self, e, eng, csems, dsems):
        waited = {}
        nwaits = 0
        for op in self.streams[e]:
            waits = []
            for d in op.deps_c.values():
                waits.append((("c", d.eng), csems[d.eng], d.sval))
            for d in op.deps_d:
                waits.append((("d",) + d.dsem, dsems[d.dsem], d.dval))
            if op.is_dma and op.prevval > 0:
                waits.append((("d",) + op.dsem, dsems[op.dsem], op.prevval))
            need = {}
            for key, sem, val in waits:
                if waited.get(key, 0) < val and need.get(key, (None, 0))[1] < val:
                    need[key] = (sem, val)
            need = list(need.items())
            emb = None
            if need and not op.is_dma:
                emb = need.pop()
            for key, (sem, val) in need:
                eng.wait_ge(sem, val)
                waited[key] = val
                nwaits += 1
            ins = op.fn(eng)
            if emb is not None:
                ins._wait_ge(emb[1][0], emb[1][1])
                waited[emb[0]] = emb[1][1]
            if op.is_dma:
                ins.then_inc(dsems[op.dsem], 16)
            elif op.signal:
                ins.then_inc(csems[e], 1)
        if e == "sp":
            for q in self.ENGS:
                for op in self.streams[q]:
                    if op.is_dma and op.tag == "out":
                        key = ("d",) + op.dsem
                        if waited.get(key, 0) < op.dval:
                            eng.wait_ge(dsems[op.dsem], op.dval)
                            waited[key] = op.dval
        return nwaits


def _t5_bucket_np(n):
    n = np.maximum(n, 0)
    max_exact = 16
    nf = np.maximum(n, 1).astype(np.float32)
    large = max_exact + (np.log(nf / np.float32(max_exact)) / np.float32(math.log(128 / 16))
                         * np.float32(32 - max_exact)).astype(np.int32)
    large = np.minimum(large, 31)
    return np.where(n < max_exact, n, large)


def _onehot_table():
    d = np.arange(TAB_L) - 511
    oh = np.zeros((33, TAB_L), np.float32)
    b = _t5_bucket_np(d)
    for i in range(TAB_L):
        if d[i] >= 0:
            oh[b[i], i] = 1.0
        else:
            oh[32, i] = 1.0
    return oh


def build_program(stop_after=None, dump_mix=False, dumps=()):
    nc = bass.Bass("TRN2", target_bir_lowering=False)
    S = Sched()
    es = ExitStack()

    def dbg_dump(name, ap, ncols, dt):
        if name not in dumps:
            return
        dd = nc.dram_tensor("dbg_" + name, [128, ncols], dt, kind="ExternalOutput")
        S.barrier()
        S.dma("sp", lambda e: e.dma_start(out=dd[:, :], in_=ap), [], [], tag="out")
        S.barrier()

    def din(name, shape, dt):
        return nc.dram_tensor(name, shape, dt, kind="ExternalInput")

    x_d = din("x", [S_LEN, D_MODEL], F32)
    w_in_d = din("w_in", [1024, 2240], F32)
    w_uq_d = din("w_uq", [384, 768], F32)
    w_ukv_d = din("w_ukv", [256, 1024], F32)
    w_out_d = din("w_out", [1024, 1024], F32)
    w1_d = din("w1", [1024, 4096], F32)
    w2_d = din("w2", [4096, 1024], F32)
    gains_d = din("gains", [128, 22], F32)
    gfin_d = din("gfin", [1, 1024], F32)
    relb_d = din("relb", [32, 4], F32)
    lam_d = din("lamv", [1, 256], F32)
    pos_d = din("pos", [128, 32], I32)
    invf_d = din("invf", [1, 32], F32)
    oh_d = din("oh", [33, TAB_L], F32)
    ident_d = din("ident", [128, 128], BF16)
    antiid_d = din("antiid", [128, 128], BF16)
    y_d = nc.dram_tensor("y", [S_LEN, D_MODEL], F32, kind="ExternalOutput")
    scr_d = nc.dram_tensor("scr", [5, TAB_L], BF16, kind="Internal")
    w1b_d = nc.dram_tensor("w1b", [1024, 4096], BF16, kind="Internal")
    w2b_d = nc.dram_tensor("w2b", [4096, 1024], BF16, kind="Internal")
    wob_d = nc.dram_tensor("wob", [1024, 1024], BF16, kind="Internal")
    mixdump_d = None
    if dump_mix:
        mixdump_d = nc.dram_tensor("mixdump", [128, 8 * S_LEN], BF16, kind="ExternalOutput")

    with es:
        def sb(name, shape, dt):
            return es.enter_context(nc.sbuf_tensor(name, shape, dt))

        ident = sb("ident_sb", [128, 128], BF16)
        antiid = sb("antiid_sb", [128, 128], BF16)
        ones = sb("ones_sb", [128, 128], BF16)
        c32 = sb("c32_sb", [128, 128], F32)
        gains = sb("gains_sb", [128, 22], F32)
        small = sb("small_sb", [128, 16], F32)
        rstd_all = sb("rstd_all_sb", [128, 32], F32)
        r_rstd = [Res(f"rstd{t}") for t in range(32)]
        strips = sb("strips_sb", [128, 5, 1024], BF16)
        mixT = sb("mixT_sb", [128, 8, S_LEN], BF16)
        ARENA_BYTES = 128 * 1024
        arena = sb("arena_sb", [128, ARENA_BYTES // 2], BF16)
        r_const = Res("const")
        r_small = Res("small")
        r_strips = Res("strips")
        r_mix = [[Res(f"mix{c}_{j}") for j in range(NQ)] for c in range(8)]

        banks = [es.enter_context(nc.psum_tensor(f"bank{i}", [128, 512], F32)) for i in range(8)]
        r_bank = [Res(f"bank{i}", True) for i in range(8)]

        def bank_bf(i):
            return banks[i][:, :].bitcast(BF16)

        class Arena:
            def __init__(self, base=None, nbytes=None):
                self.off = 0
                self.base = base
                self.nbytes = nbytes

            def reset(self, off=0):
                self.off = off

            def alloc(self, nbytes, dt, shape3=None):
                off = (self.off + 31) // 32 * 32
                self.off = off + nbytes
                if self.base is None:
                    assert self.off <= ARENA_BYTES, f"arena overflow {self.off}"
                    ap = arena[:, off // 2:(off + nbytes) // 2]
                else:
                    assert self.off <= self.nbytes, f"arena overflow {self.off}"
                    ap = self.base[:, off // 2:(off + nbytes) // 2]
                if dt == F32:
                    ap = ap.bitcast(F32)
                elif dt == I32:
                    ap = ap.bitcast(I32)
                if shape3 is not None:
                    ap = ap.rearrange("p (a b) -> p a b", a=shape3[0], b=shape3[1])
                return ap

        A = Arena()

        def mm(out, lhsT, rhs, start, stop, reads, writes):
            S.op("pe", lambda e: e.matmul(out, lhsT=lhsT, rhs=rhs, start=start, stop=stop), reads, writes)

        def tr(out, in_, reads, writes):
            S.op("pe", lambda e: e.transpose(out=out, in_=in_, identity=ident[:, :]), list(reads) + [r_const], writes)

        def act(out, in_, func, reads, writes, **kw):
            S.op("act", lambda e: e.activation(out=out, in_=in_, func=func, **kw), reads, writes)

        def tt(eng, out, in0, in1, op, reads, writes):
            S.op(eng, lambda e: e.tensor_tensor(out=out, in0=in0, in1=in1, op=op), reads, writes)

        def ts(eng, out, in0, s1, s2, op0, op1, reads, writes):
            if s2 is None:
                S.op(eng, lambda e: e.tensor_scalar(out=out, in0=in0, scalar1=s1, scalar2=None, op0=op0), reads, writes)
            else:
                S.op(eng, lambda e: e.tensor_scalar(out=out, in0=in0, scalar1=s1, scalar2=s2, op0=op0, op1=op1), reads, writes)

        def stt(eng, out, in0, scalar, in1, op0, op1, reads, writes):
            S.op(eng, lambda e: e.scalar_tensor_tensor(out=out, in0=in0, scalar=scalar, in1=in1, op0=op0, op1=op1),
                 reads, writes)

        def cp(eng, out, in_, reads, writes):
            if eng == "act":
                act(out, in_, ACT.Copy, reads, writes)
            else:
                S.op(eng, lambda e: e.tensor_copy(out=out, in_=in_), reads, writes)

        def recip(eng, out, in_, reads, writes):
            S.op(eng, lambda e: e.reciprocal(out=out, in_=in_), reads, writes)

        def dma(q, out, in_, reads, writes, tag=None):
            S.dma(q, lambda e: e.dma_start(out=out, in_=in_), reads, writes, tag=tag)

        def rsqrt_act(out, in_, n, reads, writes, tmp, r_tmp):
            act(tmp, in_, ACT.Ln, list(reads), [r_tmp], scale=1.0 / n, bias=EPS)
            act(out, tmp, ACT.Exp, [r_tmp], writes, scale=-0.5)

        A0 = Arena(mixT[:, :, :].rearrange("p a b -> p (a b)"), 8 * S_LEN * 2)
        A_main = A
        A = A0
        oh_sb = A.alloc(TAB_L * 4, F32)
        relb_aug = A.alloc(32, F32)
        lamt = A.alloc(256 * 4, F32)
        prod = A.alloc(128 * 4, F32)
        etab = A.alloc(1152 * 2, BF16)
        hank = [A.alloc(1024 * 2, BF16) for _ in range(2)]
        posi = A.alloc(32 * 4, I32)
        posf = A.alloc(32 * 4, F32)
        invf = A.alloc(32 * 4, F32)
        ang = A.alloc(1024 * 4, F32, (32, 32))
        u_t = A.alloc(1024 * 4, F32, (32, 32))
        k_i = A.alloc(1024 * 4, I32, (32, 32))
        k_f = A.alloc(1024 * 4, F32, (32, 32))
        fr = A.alloc(1024 * 4, F32, (32, 32))
        stage0_end = 0
        A = A_main
        ROPE_OFF = ARENA_BYTES - 2 * 8192
        A.reset(ROPE_OFF)
        C2 = A.alloc(8192, F32, (32, 64))
        S2 = A.alloc(8192, F32, (32, 64))
        assert stage0_end <= ROPE_OFF
        r_oh, r_relb, r_lamt, r_prod, r_etab = Res("oh"), Res("relb"), Res("lamt"), Res("prod"), Res("etab")
        r_hank = [Res("hank0"), Res("hank1")]
        r_pos, r_posf, r_invf, r_ang, r_u, r_ki, r_kf, r_fr = (Res(n) for n in
                                                                ("pos", "posf", "invf", "ang", "u", "ki", "kf", "fr"))
        r_C2, r_S2, r_scr = Res("C2"), Res("S2"), Res("scr")

        dma("sp", ident[:, :], ident_d[:, :], [], [r_const])
        dma("sp", antiid[:, :], antiid_d[:, :], [], [r_const])
        S.op("pool", lambda e: e.memset(ones[:, :], 1.0), [], [r_const])
        S.op("pool", lambda e: e.memset(c32[:, :], 1.0 / 32.0), [], [r_const])
        dma("sp", gains[:, :], gains_d[:, :], [], [r_const])
        dma("sp", oh_sb[0:33, :], oh_d[:, :], [], [r_oh])
        S.op("pool", lambda e: e.memset(relb_aug[0:33, 0:8], 0.0), [], [r_relb])
        S.op("pool", lambda e: e.memset(relb_aug[32:33, 0:8], -30000.0), [], [r_relb])
        dma("sp", relb_aug[0:32, 0:4], relb_d[:, :], [], [r_relb])
        dma("sp", small[:, 0:4], bass.AP(relb_d, 31 * 4, [[0, 128], [1, 4]]), [], [r_small])
        dma("sp", lamt, bass.AP(lam_d, 0, [[0, 128], [1, 256]]), [], [r_lamt])
        dma("sp", posi, pos_d[:, :], [], [r_pos])
        dma("sp", invf, bass.AP(invf_d, 0, [[0, 128], [1, 32]]), [], [r_invf])

        def stage0_late():
            tt("dve", prod, lamt[:, 0:128], lamt[:, 128:256], ALU.mult, [r_lamt], [r_prod])
            S.op("dve", lambda e: e.tensor_reduce(out=small[:, 8:10], in_=prod.rearrange("p (a b) -> p a b", a=2, b=64),
                                                  axis=AX.X, op=ALU.add), [r_prod], [r_small])
            act(small[:, 6:8], small[:, 8:10], ACT.Exp, [r_small], [r_small])
            tt("dve", small[:, 4:5], small[:, 7:8], small[:, 6:7], ALU.subtract, [r_small], [r_small])
            ts("dve", small[:, 4:5], small[:, 4:5], -0.2, None, ALU.add, None, [r_small], [r_small])
            ts("dve", small[:, 5:6], gains[:, 21:22], 0.8, None, ALU.mult, None, [r_const, r_small], [r_small])

            for ci, (c0, c1) in enumerate(((0, 512), (512, 1024), (1024, TAB_L))):
                mm(banks[ci][0:5, 0:c1 - c0], relb_aug[0:33, 0:5], oh_sb[0:33, c0:c1], True, True,
                   [r_relb, r_oh], [r_bank[ci]])
                act(etab[0:5, c0:c1], banks[ci][0:5, 0:c1 - c0], ACT.Exp, [r_bank[ci]], [r_etab])
            dma("sp", scr_d[:, :], etab[0:5, 0:TAB_L], [r_etab], [r_scr])
            for h in range(5):
                hk = hank[h % 2]
                dma("sp", hk, bass.AP(scr_d, h * TAB_L, [[1, 128], [1, 1024]]), [r_scr], [r_hank[h % 2]])
                for half in range(2):
                    bi = 3 + (2 * h + half) % 3
                    mm(banks[bi][:, :], antiid[:, :], hk[:, half * 512:(half + 1) * 512], True, True,
                       [r_const, r_hank[h % 2]], [r_bank[bi]])
                    cp("act" if half else "dve", strips[:, h, half * 512:(half + 1) * 512], banks[bi][:, :],
                       [r_bank[bi]], [r_strips])

        cp("dve", posf, posi, [r_pos], [r_posf])
        tt("dve", ang, invf.unsqueeze(1).to_broadcast([128, 32, 32]), posf.unsqueeze(2).to_broadcast([128, 32, 32]),
           ALU.mult, [r_invf, r_posf], [r_ang])
        TWO_PI_S = 6.283184
        for kind in ("sin", "cos"):
            if kind == "sin":
                ts("dve", u_t, ang, 1.0 / (2 * math.pi), None, ALU.mult, None, [r_ang], [r_u])
            else:
                ts("dve", u_t, ang, 1.0 / (2 * math.pi), 0.25, ALU.mult, ALU.add, [r_ang], [r_u])
            cp("dve", k_i, u_t, [r_u], [r_ki])
            cp("dve", k_f, k_i, [r_ki], [r_kf])
            tt("dve", fr, u_t, k_f, ALU.subtract, [r_u, r_kf], [r_fr])
            if kind == "sin":
                act(S2[:, :, 32:64], fr, ACT.Sin, [r_fr], [r_S2], scale=TWO_PI_S)
                act(S2[:, :, 0:32], fr, ACT.Sin, [r_fr], [r_S2], scale=-TWO_PI_S)
            else:
                act(C2[:, :, 0:32], fr, ACT.Sin, [r_fr], [r_C2], scale=TWO_PI_S)
                act(C2[:, :, 32:64], fr, ACT.Sin, [r_fr], [r_C2], scale=TWO_PI_S)

        g_attn = gains[:, 0:8]
        g_mlp = gains[:, 8:16]
        g_q = gains[:, 16:19]
        g_kv = gains[:, 19:21]

        STRIP_ENG = "dve"

        SRING = [0, 1, 2, 7]
        sb_ctr = [0]

        def next_sb():
            b = SRING[sb_ctr[0] % 4]
            sb_ctr[0] += 1
            return b

        def skew(nt, stages, hook=None):
            ns = len(stages)
            for it in range(nt + ns - 1):
                if hook is not None:
                    hook(it)
                for s in range(ns - 1, -1, -1):
                    t = it - s
                    if 0 <= t < nt:
                        stages[s](t)

        def run_attention(units, PT, r_PT):
            steps = [(ui, c) for ui, u in enumerate(units) for c in range(4 * u["j"] + 4)]
            DEPTH = 3
            NPT = len(PT)
            sbank_of = {}
            deferred = []

            def lo_of(u, c):
                m = c - 4 * u["j"]
                return 128 * m if (m > 0 and u["j"] > 0) else 0

            def emit_S(i):
                ui, c = steps[i]
                u = units[ui]
                lo = lo_of(u, c)
                bk = next_sb()
                parts = u["kparts"]
                for pi, (kfn, qap, qres) in enumerate(parts):
                    kap, kres = kfn(c)
                    mm(banks[bk][:, lo:512], kap, qap[:, lo:512], pi == 0, pi == len(parts) - 1,
                       [kres, qres], [r_bank[bk]])
                near = c >= 4 * u["j"] + u["nearmin"]
                pt = PT[i % NPT]
                if near or u["bias"] is None:
                    act(pt[:, lo:512], banks[bk][:, lo:512], ACT.Exp, [r_bank[bk]], [r_PT[i % NPT]], scale=u["scale"])
                else:
                    act(pt[:, lo:512], banks[bk][:, lo:512], ACT.Exp, [r_bank[bk], r_small], [r_PT[i % NPT]],
                        scale=u["scale"], bias=u["bias"])
                if near:
                    delta = 512 * u["j"] - 128 * c
                    st = strips[:, u["strip"], delta + 384 + lo:delta + 384 + 512]
                    tt(STRIP_ENG, pt[:, lo:512], pt[:, lo:512], st, ALU.mult, [r_PT[i % NPT], r_strips], [r_PT[i % NPT]])

            def emit_PV(i):
                ui, c = steps[i]
                u = units[ui]
                lo = lo_of(u, c)
                first = c == 0
                last = c == 4 * u["j"] + 3
                ob = 3 + 2 * (ui % 2)
                db = ob + 1
                vap, vres = u["v"](c)
                pt = PT[i % NPT]
                mm(banks[ob][:, lo:512], vap, pt[:, lo:512], first, last, [vres, r_PT[i % NPT]], [r_bank[ob]])
                if c % 4 == 3:
                    for g in range(4):
                        ii = i - 3 + g
                        cc = c - 3 + g
                        lo2 = lo_of(u, cc)
                        p2 = PT[ii % NPT]
                        S.op("pe", lambda e, g=g, lo2=lo2, p2=p2, st=(cc < 4), sp=(cc >= 4 * u["j"]), db=db: e.matmul(
                            banks[db][32 * g:32 * g + 32, lo2:512], lhsT=ones[:, 0:32], rhs=p2[:, lo2:512], start=st, stop=sp,
                            tile_position=(0, 32 * g)), [r_const, r_PT[ii % NPT]], [r_bank[db]])
                if last:
                    u["fin"](u, ob, db, lambda fn, k=3: deferred.append((i + k, fn)))

            n = len(steps)
            task_at = {}
            base = 0
            for ui, u in enumerate(units):
                ns_ = 4 * u["j"] + 4
                tk = u.get("tasks", [])
                for k, fn in enumerate(tk):
                    task_at.setdefault(base + (k * ns_) // len(tk), []).append(fn)
                base += ns_
            for i in range(n + DEPTH):
                if i < n:
                    emit_S(i)
                    for fn in task_at.get(i, []):
                        fn()
                if i >= DEPTH:
                    emit_PV(i - DEPTH)
                    due = [d for d in deferred if d[0] <= i - DEPTH]
                    for d in due:
                        deferred.remove(d)
                        d[1]()
            for d in deferred:
                d[1]()

        A.reset()
        c_qT = A.alloc(3 * S_LEN * 2, BF16, (3, S_LEN))
        c_kvT = A.alloc(2 * S_LEN * 2, BF16, (2, S_LEN))
        k_ropeT = A.alloc(S_LEN * 2, BF16)
        q_ropeT = A.alloc(2 * S_LEN * 2, BF16, (2, S_LEN))
        BOUT_END = A.off
        r_cq = [Res(f"cq{j}") for j in range(NQ)]
        r_ckv = [Res(f"ckv{j}") for j in range(NQ)]
        r_krope = [Res(f"krope{j}") for j in range(NQ)]
        r_qrope = [Res(f"qrope{j}") for j in range(NQ)]

        xs = [A.alloc(4096, F32) for _ in range(3)]
        r_xs = [Res(f"xs{i}") for i in range(3)]
        junk = A.alloc(2048, BF16)
        r_junk = Res("junk")
        xn = [A.alloc(2048, BF16) for _ in range(2)]
        r_xn = [Res(f"xn{i}") for i in range(2)]
        hTt = [A.alloc(2048, BF16, (8, 128)) for _ in range(2)]
        r_hTt = [Res(f"hTt{i}") for i in range(2)]
        Wlat = A.alloc(8 * 768 * 2, BF16, (8, 768))
        r_Wlat = [Res(f"Wlat{i}") for i in range(3)]
        Wqr = A.alloc(3 * 512 * 2, BF16, (3, 512))
        r_Wqr = [Res(f"Wqr{i}") for i in range(12)]
        latn = [A.alloc(768 * 2, BF16) for _ in range(2)]
        r_latn = [Res(f"latn{i}") for i in range(2)]
        stat = [A.alloc(64, F32) for _ in range(2)]
        r_stat = [Res(f"stat{i}") for i in range(2)]
        rtmp = [A.alloc(3 * 256, F32) for _ in range(2)]
        r_rtmp = [Res(f"rtmp{i}") for i in range(2)]
        qtmp = [A.alloc(2 * 1024, F32) for _ in range(2)]
        r_qtmp = [Res(f"qtmp{i}") for i in range(2)]
        qper = [A.alloc(512, BF16) for _ in range(2)]
        r_qper = [Res(f"qper{i}") for i in range(2)]
        assert A.off <= ROPE_OFF, A.off

        def wview(wd, c0, c1):
            return wd[:, c0:c1].rearrange("(k p) c -> p k c", p=128)

        dma("pool", Wlat[:, :, 0:704], wview(w_in_d, 1536, 2240), [], [r_Wlat[0]])
        dma("pool", Wlat[:, :, 704:736], wview(w_in_d, 2208, 2240), [], [r_Wlat[1]])
        dma("pool", Wlat[:, :, 736:768], wview(w_in_d, 2176, 2208), [], [r_Wlat[2]])
        for h in range(4):
            b = h * 192 + 128
            dma("pool", Wqr[:, :, h * 64:(h + 1) * 64], wview(w_uq_d, b, b + 64), [], [r_Wqr[3 * h]])
            dma("pool", Wqr[:, :, 256 + h * 64:256 + h * 64 + 32], wview(w_uq_d, b + 32, b + 64), [], [r_Wqr[3 * h + 1]])
            dma("pool", Wqr[:, :, 256 + h * 64 + 32:256 + h * 64 + 64], wview(w_uq_d, b, b + 32), [], [r_Wqr[3 * h + 2]])

        def norm_stages(gain_ap, dst_fn, xs, r_xs, xn, r_xn, junk, r_junk, stat, r_stat, tp_banks, reuse_rstd=False):
            def st_load(t):
                b = t % len(xs)
                dma("sp", xs[b], x_d[t * 128:(t + 1) * 128, :], [], [r_xs[b]])

            def st_stat(t):
                b = t % len(xs)
                sa = stat[t % 2]
                act(junk, xs[b], ACT.Square, [r_xs[b]], [r_junk, r_stat[t % 2]], accum_out=sa[:, 0:1])
                act(sa[:, 1:2], sa[:, 0:1], ACT.Ln, [r_stat[t % 2]], [r_stat[t % 2]], scale=1.0 / 1024.0, bias=EPS)
                act(rstd_all[:, t:t + 1], sa[:, 1:2], ACT.Exp, [r_stat[t % 2]], [r_rstd[t]], scale=-0.5)

            def st_xn(t):
                b = t % len(xs)
                ts("dve", xn[t % 2], xs[b], rstd_all[:, t:t + 1], None, ALU.mult, None, [r_xs[b], r_rstd[t]], [r_xn[t % 2]])

            def st_tr(t):
                bk = tp_banks[t % len(tp_banks)]
                tpv = bank_bf(bk).rearrange("p (a b) -> p a b", a=8, b=128)
                for c in range(8):
                    tr(tpv[:, c, :], xn[t % 2][:, c * 128:(c + 1) * 128], [r_xn[t % 2]], [r_bank[bk]])

            def st_evac(t):
                bk = tp_banks[t % len(tp_banks)]
                tpv = bank_bf(bk).rearrange("p (a b) -> p a b", a=8, b=128)
                dst, rdst = dst_fn(t)
                tt("dve", dst, tpv, gain_ap.unsqueeze(2).to_broadcast([128, 8, 128]), ALU.mult,
                   [r_bank[bk], r_const], [rdst])
            if reuse_rstd:
                return [st_load, st_xn, st_tr, st_evac]
            return [st_load, st_stat, st_xn, st_tr, st_evac]

        nst1 = norm_stages(g_attn, lambda t: (hTt[t % 2], r_hTt[t % 2]), xs, r_xs, xn, r_xn, junk, r_junk,
                           stat, r_stat, [0, 1])

        statB = [A.alloc(64, F32) for _ in range(2)]
        r_statB = [Res(f"statB{i}") for i in range(2)]
        assert A.off <= ROPE_OFF, A.off

        def st_latmm(t):
            h = hTt[t % 2]
            b0 = 2 + 2 * (t % 2)
            b1 = b0 + 1
            for k in range(8):
                mm(banks[b0][:, 0:384], h[:, k, :], Wlat[:, k, 0:384], k == 0, k == 7, [r_hTt[t % 2]] + r_Wlat, [r_bank[b0]])
            for k in range(8):
                mm(banks[b1][:, 0:384], h[:, k, :], Wlat[:, k, 384:768], k == 0, k == 7, [r_hTt[t % 2]] + r_Wlat, [r_bank[b1]])

        def st_latel(t):
            b0 = 2 + 2 * (t % 2)
            b1 = b0 + 1
            sa = statB[t % 2]
            ln = latn[t % 2]
            rs = [r_statB[t % 2]]
            act(junk[:, 0:384], banks[b0][:, 0:384], ACT.Square, [r_bank[b0]], [r_junk] + rs, accum_out=sa[:, 4:5])
            act(junk[:, 0:256], banks[b1][:, 0:256], ACT.Square, [r_bank[b1]], [r_junk] + rs, accum_out=sa[:, 5:6])
            rt = rtmp[t % 2]
            tt("dve", rt[:, 0:64], banks[b1][:, 256:320], C2[:, t, :], ALU.mult, [r_bank[b1], r_C2], [r_rtmp[t % 2]])
            tt("dve", rt[:, 64:128], banks[b1][:, 320:384], S2[:, t, :], ALU.mult, [r_bank[b1], r_S2], [r_rtmp[t % 2]])
            tt("dve", ln[:, 640:704], rt[:, 0:64], rt[:, 64:128], ALU.add, [r_rtmp[t % 2]], [r_latn[t % 2]])
            tt("dve", ln[:, 704:768], rt[:, 0:64], rt[:, 64:128], ALU.add, [r_rtmp[t % 2]], [r_latn[t % 2]])
            rsqrt_act(sa[:, 8:9], sa[:, 4:5], 384.0, rs, rs, sa[:, 6:7], r_statB[t % 2])
            rsqrt_act(sa[:, 9:10], sa[:, 5:6], 256.0, rs, rs, sa[:, 7:8], r_statB[t % 2])
            ts("dve", ln[:, 0:384], banks[b0][:, 0:384], sa[:, 8:9], None, ALU.mult, None, [r_bank[b0]] + rs, [r_latn[t % 2]])
            act(ln[:, 384:640], banks[b1][:, 0:256], ACT.Copy, [r_bank[b1]] + rs, [r_latn[t % 2]], scale=sa[:, 9:10])

        def tp2v():
            return bank_bf(6).rearrange("p (a b) -> p a b", a=8, b=128)

        def st_tr2(t):
            ln = latn[t % 2]
            tpv = tp2v()
            for c in range(6):
                tr(tpv[:, c, :], ln[:, c * 128:(c + 1) * 128], [r_latn[t % 2]], [r_bank[6]])

        def st_ev2(t):
            tpv = tp2v()
            j = t // 4
            cols = slice(t * 128, (t + 1) * 128)
            tt("dve", c_qT[:, :, cols], tpv[:, 0:3, :], g_q.unsqueeze(2).to_broadcast([128, 3, 128]), ALU.mult,
               [r_bank[6], r_const], [r_cq[j]])
            tt("dve", c_kvT[:, :, cols], tpv[:, 3:5, :], g_kv.unsqueeze(2).to_broadcast([128, 2, 128]), ALU.mult,
               [r_bank[6], r_const], [r_ckv[j]])
            cp("act", k_ropeT[:, cols], tpv[:, 5, :], [r_bank[6]], [r_krope[j]])

        def st_qpe(t):
            j = t // 4
            cols = slice(t * 128, (t + 1) * 128)
            for k in range(3):
                mm(banks[7][:, :], c_qT[:, k, cols], Wqr[:, k, :], k == 0, k == 2, [r_cq[j]] + r_Wqr, [r_bank[7]])

        def st_qrope(t):
            qt = qtmp[t % 2]
            a3 = qt[:, 0:256].rearrange("p (a b) -> p a b", a=4, b=64)
            b3 = qt[:, 256:512].rearrange("p (a b) -> p a b", a=4, b=64)
            p3 = banks[7][:, 0:256].rearrange("p (a b) -> p a b", a=4, b=64)
            s3 = banks[7][:, 256:512].rearrange("p (a b) -> p a b", a=4, b=64)
            tt("dve", a3, p3, C2[:, t, :].unsqueeze(1).to_broadcast([128, 4, 64]), ALU.mult, [r_bank[7], r_C2], [r_qtmp[t % 2]])
            tt("dve", b3, s3, S2[:, t, :].unsqueeze(1).to_broadcast([128, 4, 64]), ALU.mult, [r_bank[7], r_S2], [r_qtmp[t % 2]])
            tt("dve", qper[t % 2], qt[:, 0:256], qt[:, 256:512], ALU.add, [r_qtmp[t % 2]], [r_qper[t % 2]])

        def st_tr3(t):
            tpv = tp2v()
            for c in range(2):
                tr(tpv[:, 6 + c, :], qper[t % 2][:, c * 128:(c + 1) * 128], [r_qper[t % 2]], [r_bank[6]])

        def st_cp3(t):
            j = t // 4
            cols = slice(t * 128, (t + 1) * 128)
            tpv = tp2v()
            cp("act", q_ropeT[:, :, cols], tpv[:, 6:8, :], [r_bank[6]], [r_qrope[j]])

        skew(NT, nst1 + [st_latmm, st_latel, st_tr2, st_ev2, st_qpe, st_qrope, st_tr3, st_cp3],
             hook=lambda it: stage0_late() if it == 39 else None)
        dbg_dump("cqT", c_qT.rearrange("p a b -> p (a b)"), 3 * S_LEN, BF16)
        dbg_dump("ckvT", c_kvT.rearrange("p a b -> p (a b)"), 2 * S_LEN, BF16)
        dbg_dump("kropeT", k_ropeT, S_LEN, BF16)
        dbg_dump("qropeT", q_ropeT.rearrange("p a b -> p (a b)"), 2 * S_LEN, BF16)
        dbg_dump("C2", C2.rearrange("p a b -> p (a b)"), 2048, F32)
        dbg_dump("S2", S2.rearrange("p a b -> p (a b)"), 2048, F32)

        S.barrier()
        A.reset(BOUT_END)
        q_nT = A.alloc(S_LEN * 2, BF16)
        k_nT = A.alloc(S_LEN * 2, BF16)
        Vh = A.alloc(S_LEN * 2, BF16, (32, 128))
        r_qn = [Res(f"qn{j}") for j in range(NQ)]
        r_kn = [Res(f"kn{j}") for j in range(NQ)]
        r_V = [Res(f"V{j}") for j in range(NQ)]
        Wuqn = A.alloc(3 * 512 * 2, BF16, (3, 512))
        Wukn = A.alloc(2 * 512 * 2, BF16, (2, 512))
        Wukv = A.alloc(2 * 512 * 2, BF16, (2, 512))
        r_Wm = [Res(f"Wmla{i}") for i in range(12)]
        PT = [A.alloc(1024, BF16) for _ in range(8)]
        r_PT = [Res(f"PT{i}") for i in range(8)]
        Rt = [A.alloc(2048, F32) for _ in range(2)]
        r_Rt = [Res(f"Rt{i}") for i in range(2)]
        Dsb = [A.alloc(2048, F32) for _ in range(2)]
        r_Dsb = [Res(f"Dsb{i}") for i in range(2)]
        kr_hi = A.alloc(S_LEN * 2, BF16)
        kr_lo = k_ropeT
        S.op("dve", lambda e: e.memset(kr_hi[0:64, :], 0.0), [], r_krope)
        S.op("dve", lambda e: e.tensor_copy(out=kr_hi[64:128, :], in_=k_ropeT[64:128, :]), r_krope, r_krope)
        S.op("dve", lambda e: e.memset(k_ropeT[64:128, :], 0.0), r_krope, r_krope)
        kr_pad = [kr_lo, kr_hi]
        for h in range(4):
            dma("pool", Wuqn[:, :, h * 128:(h + 1) * 128], wview(w_uq_d, h * 192, h * 192 + 128), [], [r_Wm[3 * h]])
            dma("pool", Wukn[:, :, h * 128:(h + 1) * 128], wview(w_ukv_d, h * 256, h * 256 + 128), [], [r_Wm[3 * h + 1]])
            dma("pool", Wukv[:, :, h * 128:(h + 1) * 128], wview(w_ukv_d, h * 256 + 128, h * 256 + 256), [], [r_Wm[3 * h + 2]])

        r_w1b = [Res(f"w1b{i}") for i in range(8)]
        r_w2b = [Res(f"w2b{i}") for i in range(8)]
        r_wob = [Res(f"wob{i}") for i in range(2)]
        for i in range(2):
            dma("pool", wob_d[i * 512:(i + 1) * 512, :], w_out_d[i * 512:(i + 1) * 512, :], [], [r_wob[i]])
        for i in range(8):
            dma("pool", w1b_d[i * 128:(i + 1) * 128, :], w1_d[i * 128:(i + 1) * 128, :], [], [r_w1b[i]])
        for i in range(8):
            dma("pool", w2b_d[i * 512:(i + 1) * 512, :], w2_d[i * 512:(i + 1) * 512, :], [], [r_w2b[i]])

        proj_ring = [0, 1, 2, 7]
        pr_ctr = [0]

        def next_pb():
            return next_sb()

        fin_ctr = [0]
        MLA_SCALE = 192.0 ** -0.5
        for h in range(4):
            def mla_tasks(j, h=h):
                cols = slice(j * 512, (j + 1) * 512)

                def t_q():
                    b = next_pb()
                    for k in range(3):
                        mm(banks[b][:, :], Wuqn[:, k, h * 128:(h + 1) * 128], c_qT[:, k, cols], k == 0, k == 2,
                           r_Wm + [r_cq[j]], [r_bank[b]])
                    cp("dve", q_nT[:, cols], banks[b][:, :], [r_bank[b]], [r_qn[j]])

                def t_k():
                    b = next_pb()
                    for k in range(2):
                        mm(banks[b][:, :], Wukn[:, k, h * 128:(h + 1) * 128], c_kvT[:, k, cols], k == 0, k == 1,
                           r_Wm + [r_ckv[j]], [r_bank[b]])
                    cp("dve", k_nT[:, cols], banks[b][:, :], [r_bank[b]], [r_kn[j]])

                def t_v():
                    b = next_pb()
                    bv = banks[b][:, :].rearrange("p (a b) -> p a b", a=4, b=128)
                    for tt_ in range(4):
                        t = 4 * j + tt_
                        for k in range(2):
                            mm(bv[:, tt_, :], c_kvT[:, k, t * 128:(t + 1) * 128], Wukv[:, k, h * 128:(h + 1) * 128],
                               k == 0, k == 1, r_Wm + [r_ckv[j]], [r_bank[b]])
                    cp("dve", Vh[:, 4 * j:4 * j + 4, :], bv, [r_bank[b]], [r_V[j]])
                return [t_q, t_k, t_v]

            for fn in mla_tasks(0):
                fn()

            half = (h % 2) * 64
            pair = h // 2

            def fin_mla(u, ob, db, later, h=h):
                j = u["j"]
                cols = slice(j * 512, (j + 1) * 512)
                f = fin_ctr[0] % 2
                fin_ctr[0] += 1
                cp("dve", Dsb[f], banks[db][:, :], [r_bank[db]], [r_Dsb[f]])

                def tail(f=f, j=j, cols=cols, ob=ob):
                    mb = next_sb()
                    mm(banks[mb][:, :], c32[:, :], Dsb[f], True, True, [r_const, r_Dsb[f]], [r_bank[mb]])
                    act(Rt[f], banks[mb][:, :], ACT.Ln, [r_bank[mb]], [r_Rt[f]])
                    act(Rt[f], Rt[f], ACT.Exp, [r_Rt[f]], [r_Rt[f]], scale=-1.0)
                    tt("dve", mixT[:, 4 + h, cols], banks[ob][:, :], Rt[f], ALU.mult, [r_bank[ob], r_Rt[f]], [r_mix[4 + h][j]])
                later(tail, 2)

            units = []
            for j in range(NQ):
                cols = slice(j * 512, (j + 1) * 512)
                units.append(dict(
                    j=j, scale=MLA_SCALE, bias=None, strip=4, nearmin=0,
                    kparts=[
                        (lambda c: (k_nT[:, c * 128:(c + 1) * 128], r_kn[c // 4]), q_nT[:, cols], r_qn[j]),
                        (lambda c, hh=h % 2: (kr_pad[hh][:, c * 128:(c + 1) * 128], r_krope[c // 4]),
                         q_ropeT[:, pair, cols], r_qrope[j]),
                    ],
                    v=lambda c: (Vh[:, c, :], r_V[c // 4]),
                    tasks=(mla_tasks(j + 1) if j + 1 < NQ else []),
                    fin=fin_mla))
            run_attention(units, PT, r_PT)
            if h == 0:
                dbg_dump("qnT", q_nT, S_LEN, BF16)
                dbg_dump("knT", k_nT, S_LEN, BF16)
                dbg_dump("Vh", Vh.rearrange("p a b -> p (a b)"), S_LEN, BF16)

        S.barrier()
        A.reset()
        hT = A.alloc(8 * S_LEN * 2, BF16, (8, S_LEN))
        r_hT = [Res(f"hT{j}") for j in range(NQ)]
        HT_END = A.off
        xs = [A.alloc(4096, F32) for _ in range(6)]
        r_xs = [Res(f"xsb{i}") for i in range(6)]
        junk = A.alloc(2048, BF16)
        r_junk = Res("junkb")
        xn = [A.alloc(2048, BF16) for _ in range(2)]
        r_xn = [Res(f"xnb{i}") for i in range(2)]
        stat = [A.alloc(64, F32) for _ in range(2)]
        r_stat = [Res(f"statb{i}") for i in range(2)]
        assert A.off <= HT_END + 4 * S_LEN * 2, A.off
        A.reset(HT_END + 4 * S_LEN * 2)
        Wh = A.alloc(8 * 384 * 2, BF16, (8, 384))
        r_Wh = [Res(f"Wh{i}") for i in range(3)]

        def load_Wh(h):
            for part in range(3):
                c0 = part * 512 + h * 128
                dma("pool", Wh[:, :, part * 128:(part + 1) * 128], wview(w_in_d, c0, c0 + 128), [], [r_Wh[part]])

        load_Wh(0)
        nst2 = norm_stages(g_attn, lambda t: (hT[:, :, t * 128:(t + 1) * 128], r_hT[t // 4]), xs, r_xs, xn, r_xn,
                           junk, r_junk, stat, r_stat, [0, 1], reuse_rstd=True)
        skew(NT, nst2)

        S.barrier()
        A.reset(HT_END)
        qT = A.alloc(S_LEN * 2, BF16)
        kT0 = A.alloc(S_LEN * 2, BF16)
        kT1 = A.alloc(S_LEN * 2, BF16)
        kTs = [kT0, kT1]
        Vd = A.alloc(S_LEN * 2, BF16, (32, 128))
        r_q = [Res(f"q{j}") for j in range(NQ)]
        r_k = [Res(f"k{j}") for j in range(NQ)]
        r_Vd = [Res(f"Vd{j}") for j in range(NQ)]
        _wh2 = A.alloc(8 * 384 * 2, BF16, (8, 384))
        PT = [A.alloc(1024, BF16) for _ in range(8)]
        r_PT = [Res(f"PTd{i}") for i in range(8)]
        On = [A.alloc(2048, F32) for _ in range(3)]
        r_On = [Res(f"On{i}") for i in range(3)]
        Dsb = A.alloc(2048, F32)
        r_Dsbd = Res("Dsbd")
        Ct = [A.alloc(2048, F32) for _ in range(2)]
        r_Ct = [Res(f"Ct{i}") for i in range(2)]
        sqt = [A.alloc(1024, BF16) for _ in range(2)]
        r_sqt = [Res(f"sqt{i}") for i in range(2)]
        lnt = [A.alloc(2048, F32) for _ in range(2)]
        r_lnt = [Res(f"lnt{i}") for i in range(2)]
        S.op("dve", lambda e: e.memset(kT0[64:128, :], 0.0), [], r_k)
        S.op("dve", lambda e: e.memset(kT1[0:64, :], 0.0), [], r_k)

        pair_ctr = [0]
        DIFF_SCALE = 64.0 ** -0.5
        for h in range(4):
            def diff_tasks(j, h=h):
                w = Wh
                rw = r_Wh
                cols = slice(j * 512, (j + 1) * 512)

                def t_q():
                    b = next_pb()
                    for k in range(8):
                        mm(banks[b][:, :], w[:, k, 0:128], hT[:, k, cols], k == 0, k == 7, rw + [r_hT[j]], [r_bank[b]])
                    cp("dve", qT[:, cols], banks[b][:, :], [r_bank[b]], [r_q[j]])

                def t_k():
                    b = next_pb()
                    for k in range(8):
                        mm(banks[b][:, :], w[:, k, 128:256], hT[:, k, cols], k == 0, k == 7, rw + [r_hT[j]], [r_bank[b]])
                    cp("dve", kT0[0:64, cols], banks[b][0:64, :], [r_bank[b]], [r_k[j]])
                    cp("dve", kT1[64:128, cols], banks[b][64:128, :], [r_bank[b]], [r_k[j]])

                def t_v(lo_, hi_):
                    def f():
                        b = next_pb()
                        bv = banks[b][:, :].rearrange("p (a b) -> p a b", a=4, b=128)
                        for tt_ in range(lo_, hi_):
                            t = 4 * j + tt_
                            for k in range(8):
                                mm(bv[:, tt_, :], hT[:, k, t * 128:(t + 1) * 128], w[:, k, 256:384], k == 0, k == 7,
                                   rw + [r_hT[j]], [r_bank[b]])
                        cp("dve", Vd[:, 4 * j + lo_:4 * j + hi_, :], bv[:, lo_:hi_, :], [r_bank[b]], [r_Vd[j]])
                    return f
                return [t_q, t_k, t_v(0, 2), t_v(2, 4)]

            for fn in diff_tasks(0):
                fn()

            def fin_diff(u, ob, db, later, h=h):
                j = u["j"]
                comp = u["comp"]
                cols = slice(j * 512, (j + 1) * 512)
                pi = pair_ctr[0] % 2
                oi = pi if comp == 0 else 2
                if comp == 1:
                    pair_ctr[0] += 1
                cp("dve", Dsb, banks[db][:, :], [r_bank[db]], [r_Dsbd])

                def tail(pi=pi, oi=oi, comp=comp, j=j, cols=cols, ob=ob):
                    mb = next_sb()
                    mm(banks[mb][:, :], c32[:, :], Dsb, True, True, [r_const, r_Dsbd], [r_bank[mb]])
                    act(On[oi], banks[mb][:, :], ACT.Ln, [r_bank[mb]], [r_On[oi]])
                    act(On[oi], On[oi], ACT.Exp, [r_On[oi]], [r_On[oi]], scale=-1.0)
                    tt("dve", On[oi], banks[ob][:, :], On[oi], ALU.mult, [r_bank[ob], r_On[oi]], [r_On[oi]])
                    if comp == 1:
                        C = Ct[pi]
                        stt("dve", C, On[2], small[:, 4:5], On[pi], ALU.mult, ALU.add,
                            [r_On[2], r_On[pi], r_small], [r_Ct[pi]])
                        tt("dve", sqt[pi], C, C, ALU.mult, [r_Ct[pi]], [r_sqt[pi]])

                        def tail2(pi=pi, j=j, cols=cols):
                            mb = next_sb()
                            mm(banks[mb][:, :], ones[:, :], sqt[pi], True, True, [r_const, r_sqt[pi]], [r_bank[mb]])
                            act(lnt[pi], banks[mb][:, :], ACT.Ln, [r_bank[mb]], [r_lnt[pi]], scale=1.0 / 128.0, bias=EPS)
                            act(lnt[pi], lnt[pi], ACT.Exp, [r_lnt[pi]], [r_lnt[pi]], scale=-0.5)
                            stt("dve", mixT[:, h, cols], Ct[pi], small[:, 5:6], lnt[pi], ALU.mult, ALU.mult,
                                [r_Ct[pi], r_small, r_lnt[pi]], [r_mix[h][j]])
                        later(tail2, 6)
                later(tail, 2)

            units = []
            for j in range(NQ):
                cols = slice(j * 512, (j + 1) * 512)
                for comp in range(2):
                    units.append(dict(
                        j=j, comp=comp, scale=DIFF_SCALE, bias=small[:, h:h + 1], strip=h, nearmin=-1,
                        kparts=[(lambda c, comp=comp: (kTs[comp][:, c * 128:(c + 1) * 128], r_k[c // 4]),
                                 qT[:, cols], r_q[j])],
                        v=lambda c: (Vd[:, c, :], r_Vd[c // 4]),
                        tasks=(((diff_tasks(j + 1)[2 * comp:2 * comp + 2]) if j + 1 < NQ else [])
                               + ([lambda h=h: load_Wh(h + 1)] if (j + 2 == NQ and comp == 1 and h + 1 < 4) else [])),
                        fin=fin_diff))
            run_attention(units, PT, r_PT)

        if dump_mix:
            S.barrier()
            dma("sp", mixdump_d[:, :], mixT[:, :, :].rearrange("p a b -> p (a b)"), [r for rr in r_mix for r in rr], [],
                tag="out")

        A.reset()
        Wo = A.alloc(8 * 1024 * 2, BF16, (8, 1024))
        r_Wo = Res("Wo")
        xb = [A.alloc(4096, F32) for _ in range(8)]
        r_xb = [Res(f"xb{i}") for i in range(8)]
        xn = [A.alloc(2048, BF16) for _ in range(2)]
        r_xn = [Res(f"xnc{i}") for i in range(2)]
        junk = A.alloc(2048, BF16)
        r_junk = Res("junkc")
        h2T = A.alloc(8 * 512 * 2, BF16, (8, 512))
        r_h2T = [Res(f"h2T{s}") for s in range(4)]
        uT = A.alloc(32 * 512 * 2, BF16, (32, 512))
        r_uT = [Res(f"uT{c}") for c in range(32)]
        rl = [A.alloc(2048, F32) for _ in range(2)]
        r_rl = [Res("rl0"), Res("rl1")]
        W1s = [A.alloc(8 * 512 * 2, BF16, (8, 512)) for _ in range(2)]
        r_W1s = [Res("W1s0"), Res("W1s1")]
        W2s = [A.alloc(4 * 512 * 2, BF16, (4, 512)) for _ in range(2)]
        r_W2s = [Res("W2s0"), Res("W2s1")]
        gfin = A.alloc(4096, F32)
        r_gfin = Res("gfin")
        stat = [A.alloc(64, F32) for _ in range(2)]
        r_stat = [Res(f"statc{i}") for i in range(2)]

        dma("sp", Wo, wview(wob_d, 0, 1024), r_wob, [r_Wo] + r_hT)
        for s_ in range(4):
            dma("pool", xb[s_], x_d[s_ * 128:(s_ + 1) * 128, :], [], [r_xb[s_]] + r_hT)
        S.barrier()
        dma("sp", gfin, bass.AP(gfin_d, 0, [[0, 128], [1, 1024]]), [], [r_gfin])

        w1_ctr = [0]
        w2_ctr = [0]
        ffn_ring = [0]
        st_ctr = [0]

        def op_load(i):
            for s_ in range(4):
                t = 4 * i + s_
                bi = (4 * i + s_) % 8
                dma("pool", xb[bi], x_d[t * 128:(t + 1) * 128, :], [], [r_xb[bi]])

        def op_a(i, s_):
            t = 4 * i + s_
            bi = (4 * i + s_) % 8
            buf = xb[bi]
            for half in range(2):
                b = 4 + half
                for k in range(8):
                    mm(banks[b][:, :], mixT[:, k, t * 128:(t + 1) * 128], Wo[:, k, half * 512:(half + 1) * 512],
                       k == 0, k == 7, [r_mix[k][i], r_Wo], [r_bank[b]])
                hs = slice(half * 512, (half + 1) * 512)
                tt("dve", buf[:, hs], buf[:, hs], banks[b][:, :], ALU.add, [r_xb[bi], r_bank[b]], [r_xb[bi]])

        def op_b(i, s_):
            bi = (4 * i + s_) % 8
            buf = xb[bi]
            si = st_ctr[0] % 2
            st_ctr[0] += 1
            sa = stat[si]
            rs = [r_stat[si]]
            act(junk, buf, ACT.Square, [r_xb[bi]], [r_junk] + rs, accum_out=sa[:, 0:1])
            rsqrt_act(sa[:, 2:3], sa[:, 0:1], 1024.0, rs, rs, sa[:, 1:2], r_stat[si])
            ts("dve", xn[s_ % 2], buf, sa[:, 2:3], None, ALU.mult, None, [r_xb[bi]] + rs, [r_xn[s_ % 2]])

        def op_c(i, s_):
            tpv = bank_bf(6).rearrange("p (a b) -> p a b", a=8, b=128)
            for c in range(8):
                tr(tpv[:, c, :], xn[s_ % 2][:, c * 128:(c + 1) * 128], [r_xn[s_ % 2]], [r_bank[6]])
            tt("dve", h2T[:, :, s_ * 128:(s_ + 1) * 128], tpv, g_mlp.unsqueeze(2).to_broadcast([128, 8, 128]), ALU.mult,
               [r_bank[6], r_const], [r_h2T[s_]])

        OP_SLOTS = {0: [("a", 0)], 1: [("b", 0), ("a", 1)], 2: [("c", 0), ("b", 1), ("a", 2)],
                    3: [("c", 1), ("b", 2), ("a", 3)], 4: [("c", 2), ("b", 3)], 5: [("c", 3)]}

        def op_slot(i, slot):
            for kind, s_ in OP_SLOTS.get(slot, []):
                {"a": op_a, "b": op_b, "c": op_c}[kind](i, s_)

        def FFN1(i):
            if i + 1 < NQ:
                op_load(i + 1)
            for g in range(8):
                wi = w1_ctr[0] % 2
                w1_ctr[0] += 1
                dma("sp", W1s[wi], wview(w1b_d, g * 512, (g + 1) * 512), r_w1b, [r_W1s[wi]])
                for cc in range(4):
                    c = 4 * g + cc
                    b = ffn_ring[0] % 8
                    ffn_ring[0] += 1
                    for k in range(8):
                        mm(banks[b][:, :], W1s[wi][:, k, cc * 128:(cc + 1) * 128], h2T[:, k, :], k == 0, k == 7,
                           [r_W1s[wi]] + r_h2T, [r_bank[b]])
                    ri = c % 2
                    act(rl[ri], banks[b][:, :], ACT.Relu, [r_bank[b]], [r_rl[ri]])
                    tt("dve", uT[:, c, :], rl[ri], rl[ri], ALU.mult, [r_rl[ri]], [r_uT[c]])

        def FFN2(i):
            for half in range(2):
                hs = slice(half * 512, (half + 1) * 512)
                for g in range(8):
                    wi = w2_ctr[0] % 2
                    w2_ctr[0] += 1
                    dma("sp", W2s[wi], w2b_d[g * 512:(g + 1) * 512, hs].rearrange("(k p) c -> p k c", p=128), [r_w2b[g]],
                        [r_W2s[wi]])
                    for s in range(4):
                        b = 4 * half + s
                        for cc in range(4):
                            c = 4 * g + cc
                            mm(banks[b][:, :], uT[:, c, s * 128:(s + 1) * 128], W2s[wi][:, cc, :], c == 0, c == 31,
                               [r_uT[c], r_W2s[wi]], [r_bank[b]])
                    if half == 0 and i + 1 < NQ:
                        op_slot(i + 1, g)
                for s in range(4):
                    b = 4 * half + s
                    bi = (4 * i + s) % 8
                    tt("dve", xb[bi][:, hs], xb[bi][:, hs], banks[b][:, :], ALU.add, [r_xb[bi], r_bank[b]], [r_xb[bi]])
            for s in range(4):
                t = 4 * i + s
                bi = (4 * i + s) % 8
                buf = xb[bi]
                si = st_ctr[0] % 2
                st_ctr[0] += 1
                sa = stat[si]
                rs = [r_stat[si]]
                act(junk, buf, ACT.Square, [r_xb[bi]], [r_junk] + rs, accum_out=sa[:, 0:1])
                rsqrt_act(sa[:, 2:3], sa[:, 0:1], 1024.0, rs, rs, sa[:, 1:2], r_stat[si])
                stt("dve", buf, buf, sa[:, 2:3], gfin, ALU.mult, ALU.mult, [r_xb[bi], r_gfin] + rs, [r_xb[bi]])
                dma("pool", y_d[t * 128:(t + 1) * 128, :], buf, [r_xb[bi]], [], tag="out")

        for slot in range(6):
            op_slot(0, slot)
        for i in range(NQ):
            FFN1(i)
            FFN2(i)

        S.finalize()
        csems = {e: es.enter_context(nc.semaphore("c_" + e)) for e in ("pe", "act", "dve", "pool")}
        dsems = {}
        for q, n in S.nslots.items():
            for i in range(n):
                dsems[(q, i)] = es.enter_context(nc.semaphore(f"d_{q}_{i}"))
        block = es.enter_context(nc.Block())

        @block.tensor
        def _(e):
            S.emit_engine("pe", e, csems, dsems)

        @block.scalar
        def _(e):
            S.emit_engine("act", e, csems, dsems)

        @block.vector
        def _(e):
            S.emit_engine("dve", e, csems, dsems)

        @block.gpsimd
        def _(e):
            S.emit_engine("pool", e, csems, dsems)

        @block.sync
        def _(e):
            S.emit_engine("sp", e, csems, dsems)

    return nc, S


def _col_layout(v, nch):
    return np.ascontiguousarray(np.asarray(v, np.float32).reshape(nch, 128).T)


def make_in_maps(inputs):
    f = lambda a: np.ascontiguousarray(np.asarray(a, np.float32))
    x = f(inputs["x"])
    gains = np.concatenate([
        _col_layout(inputs["norm_attn"][0], 8), _col_layout(inputs["norm_mlp"][0], 8),
        _col_layout(inputs["mla_q_norm"][0], 3), _col_layout(inputs["mla_kv_norm"][0], 2),
        _col_layout(inputs["diff_subln"][0], 1)], axis=1)
    lamv = np.concatenate([f(inputs["diff_lq1"][0]), f(inputs["diff_lq2"][0]),
                           f(inputs["diff_lk1"][0]), f(inputs["diff_lk2"][0])])[None, :]
    pos = np.ascontiguousarray(np.asarray(inputs["positions"], np.int32).reshape(NT, 128).T)
    invf = (np.float32(10000.0) ** (-np.arange(0, 64, 2, dtype=np.float32) / np.float32(64)))[None, :].astype(np.float32)
    shared = {
        "w_in": f(inputs["w_in"][0]), "w_uq": f(inputs["mla_w_uq"][0]), "w_ukv": f(inputs["mla_w_ukv"][0]),
        "w_out": f(inputs["w_out"][0]), "w1": f(inputs["w_mlp_in"][0]), "w2": f(inputs["w_mlp_out"][0]),
        "gains": np.ascontiguousarray(gains), "gfin": f(inputs["norm_final"])[None, :],
        "relb": f(inputs["rel_bias"]), "lamv": np.ascontiguousarray(lamv), "pos": pos, "invf": invf,
        "oh": _onehot_table(), "ident": np.eye(128, dtype=np.float32).astype(ml_dtypes.bfloat16),
        "antiid": np.ascontiguousarray(np.eye(128, dtype=np.float32)[::-1]).astype(ml_dtypes.bfloat16),
    }
    return [dict(shared, x=np.ascontiguousarray(x[b])) for b in range(x.shape[0])]


def kernel(**inputs):
    in_maps = make_in_maps(inputs)
    nc, _ = build_program()
    res = run_bass_kernel_spmd(nc, in_maps, core_ids=list(range(8)))
    return np.stack([np.asarray(r["y"], np.float32) for r in res.results], axis=0)
```
